# Optimizing a Trainium2 kernel written in Bass

```python
import jax, jax.numpy as jnp
from jax import lax
import numpy as np

D_MODEL = 2048
BATCH = 4
SEQ = 2048
DEPTH = 1
DEC_BATCH = 8
DEC_SEQ = 4
PAST_LEN = 16384
PAGE_SIZE = 128

SSD_HEADS = 16
SSD_HEADDIM = 64
SSD_INNER = SSD_HEADS * SSD_HEADDIM
SSD_GROUPS = 2
SSD_STATE = 128
CONV_W = 4
CONV_DIM = SSD_INNER + 2 * SSD_GROUPS * SSD_STATE
SSD_CHUNK = 128
ATT_HEADS = 16
ATT_KV_HEADS = 4
HEAD_DIM = 64
ATT_GQ = ATT_HEADS // ATT_KV_HEADS
ATT_INNER = ATT_HEADS * HEAD_DIM
ROT_DIM = HEAD_DIM // 4
ROPE_THETA = 500000.0
DILATED_BRANCHES = ((128, 1), (512, 4), (2048, 16))
W_MAX = 2048
Q_BLOCK = 128
D_MIX = SSD_INNER + ATT_INNER
EPS = 1e-6
IN_SPLITS = (SSD_INNER,
             SSD_INNER + CONV_DIM,
             SSD_INNER + CONV_DIM + SSD_HEADS,
             SSD_INNER + CONV_DIM + SSD_HEADS + ATT_INNER,
             SSD_INNER + CONV_DIM + SSD_HEADS + ATT_INNER + ATT_KV_HEADS * HEAD_DIM,
             SSD_INNER + CONV_DIM + SSD_HEADS + ATT_INNER + 2 * ATT_KV_HEADS * HEAD_DIM)
IN_COLS = SSD_INNER + CONV_DIM + SSD_HEADS + ATT_INNER + 2 * ATT_KV_HEADS * HEAD_DIM + ATT_INNER

kernel_name = "hymba_ssd_dilated_swa_step"


def rmsnorm(x, w):
    xf = x.astype(jnp.float32)
    y = xf * lax.rsqrt(jnp.mean(xf * xf, axis=-1, keepdims=True) + EPS)
    return y * w.astype(jnp.float32)


def rope_partial(x, pos):
    half = ROT_DIM // 2
    inv = ROPE_THETA ** (-jnp.arange(0, ROT_DIM, 2, dtype=jnp.float32) / ROT_DIM)
    ang = pos.astype(jnp.float32)[:, None] * inv[None, :]
    cos = jnp.cos(ang)[None, :, None, :]
    sin = jnp.sin(ang)[None, :, None, :]
    x1 = x[..., :half]
    x2 = x[..., half:ROT_DIM]
    return jnp.concatenate([x1 * cos - x2 * sin, x2 * cos + x1 * sin, x[..., ROT_DIM:]], axis=-1)


def causal_conv(xbc, conv_prev, conv_w, conv_b):
    T = xbc.shape[1]
    xp = jnp.concatenate([conv_prev.astype(xbc.dtype), xbc], axis=1)
    y = sum(conv_w[k] * xp[:, k:k + T] for k in range(CONV_W)) + conv_b
    return jax.nn.silu(y), xp[:, T:]


def ssd_chunked(x, dt, a, bm, cm, h0, chunk):
    Bsz, T, H, P = x.shape
    G, N = bm.shape[2], bm.shape[3]
    R = H // G
    nc = T // chunk
    x = x.reshape(Bsz, nc, chunk, G, R, P)
    dt = dt.reshape(Bsz, nc, chunk, G, R)
    bm = bm.reshape(Bsz, nc, chunk, G, N)
    cm = cm.reshape(Bsz, nc, chunk, G, N)
    cum = jnp.cumsum(dt * a.reshape(G, R), axis=2)
    causal = jnp.tril(jnp.ones((chunk, chunk), dtype=bool))[:, :, None, None]
    seg = cum[:, :, :, None] - cum[:, :, None, :]
    decay = jnp.exp(jnp.where(causal, seg, -jnp.inf))
    cb = jnp.einsum('bclgn,bcsgn->bclsg', cm, bm)
    scores = cb[..., None] * decay * dt[:, :, None]
    y_diag = jnp.einsum('bclsgr,bcsgrp->bclgrp', scores, x)
    decay_end = jnp.exp(cum[:, :, -1:] - cum)
    st = jnp.einsum('bcsgn,bcsgrp->bcgrpn', bm, x * (decay_end * dt)[..., None])
    chunk_decay = jnp.exp(cum[:, :, -1])

    def step(h, inp):
        s_c, d_c = inp
        return h * d_c[..., None, None] + s_c, h

    h_final, h_prev = lax.scan(step, h0.astype(jnp.float32).reshape(Bsz, G, R, P, N),
                               (jnp.moveaxis(st, 1, 0), jnp.moveaxis(chunk_decay, 1, 0)))
    h_prev = jnp.moveaxis(h_prev, 0, 1)
    y_off = jnp.einsum('bclgn,bcgrpn->bclgrp', cm, h_prev) * jnp.exp(cum)[..., None]
    y = (y_diag + y_off).reshape(Bsz, T, H, P)
    return y, h_final.reshape(Bsz, H, P, N)


def dilated_window_attention(q, k_ext, v_ext, q_idx, lo):
    nums, dens, maxs = [], [], []
    for window, dil in DILATED_BRANCHES:
        offs = jnp.arange(window // dil + 1, dtype=jnp.int32) * dil
        idx = q_idx[:, None] - offs[None, :]
        valid = idx >= lo
        idx = jnp.maximum(idx, 0)
        kg = jnp.take(k_ext, idx, axis=1)
        vg = jnp.take(v_ext, idx, axis=1)
        s = jnp.einsum('bqhgd,bqjhd->bqhgj', q, kg).astype(jnp.float32)
        s = jnp.where(valid[None, :, None, None, :], s, -jnp.inf)
        m = jnp.max(s, axis=-1)
        p = jnp.exp(s - m[..., None])
        maxs.append(m)
        dens.append(jnp.sum(p, axis=-1))
        nums.append(jnp.einsum('bqhgj,bqjhd->bqhgd', p, vg.astype(jnp.float32)))
    mx = jnp.max(jnp.stack(maxs), axis=0)
    ws = [jnp.exp(m - mx) for m in maxs]
    num = sum(n * w[..., None] for n, w in zip(nums, ws))
    den = sum(d * w for d, w in zip(dens, ws))
    return num / den[..., None]


def prompt_attention(q, k, v):
    B, T = q.shape[0], q.shape[1]
    k_pad = jnp.pad(k, ((0, 0), (W_MAX, 0), (0, 0), (0, 0)))
    v_pad = jnp.pad(v, ((0, 0), (W_MAX, 0), (0, 0), (0, 0)))
    nb = T // Q_BLOCK
    q_blocks = jnp.moveaxis(q.reshape(B, nb, Q_BLOCK, ATT_KV_HEADS, ATT_GQ, HEAD_DIM), 1, 0)
    starts = jnp.arange(nb, dtype=jnp.int32) * Q_BLOCK

    def one_block(args):
        q_blk, start = args
        q_idx = W_MAX + start + jnp.arange(Q_BLOCK, dtype=jnp.int32)
        return dilated_window_attention(q_blk, k_pad, v_pad, q_idx, W_MAX)

    o = lax.map(one_block, (q_blocks, starts))
    o = jnp.moveaxis(o, 0, 1).reshape(B, T, ATT_KV_HEADS, ATT_GQ, HEAD_DIM)
    keep = min(W_MAX, T)
    return o, k[:, T - keep:], v[:, T - keep:]


def sample_attention(q, k, v, cache_k, cache_v):
    win = cache_k.shape[1]
    Ts = q.shape[1]
    k_ext = jnp.concatenate([cache_k.astype(k.dtype), k], axis=1)
    v_ext = jnp.concatenate([cache_v.astype(v.dtype), v], axis=1)
    q_idx = win + jnp.arange(Ts, dtype=jnp.int32)
    o = dilated_window_attention(q, k_ext, v_ext, q_idx, 0)
    return o, k_ext[:, Ts:], v_ext[:, Ts:]


def hybrid_layer(x, pos, conv_prev, h0, ssd_chunk, attend, norm_w, w_in, conv_w, conv_b,
                 dt_bias, a_log, d_skip, ssd_norm_w, q_norm_w, k_norm_w, w_out):
    B, T, _ = x.shape
    hn = rmsnorm(x, norm_w)
    proj = hn @ w_in.astype(jnp.float32)
    z, xbc, dt_raw, q, k, v, gate = jnp.split(proj, IN_SPLITS, axis=-1)
    xbc_c, conv_new = causal_conv(xbc, conv_prev, conv_w, conv_b)
    xs, bm, cm = jnp.split(xbc_c, [SSD_INNER, SSD_INNER + SSD_GROUPS * SSD_STATE], axis=-1)
    dt = jax.nn.softplus(dt_raw + dt_bias)
    a = -jnp.exp(a_log.astype(jnp.float32))
    xh = xs.reshape(B, T, SSD_HEADS, SSD_HEADDIM)
    y_ssd, h_new = ssd_chunked(xh, dt, a, bm.reshape(B, T, SSD_GROUPS, SSD_STATE),
                               cm.reshape(B, T, SSD_GROUPS, SSD_STATE), h0, ssd_chunk)
    y_ssd = (y_ssd + d_skip[:, None] * xh).reshape(B, T, SSD_INNER) * jax.nn.silu(z)
    y_ssd = rmsnorm(y_ssd.reshape(B, T, SSD_GROUPS, SSD_INNER // SSD_GROUPS),
                    ssd_norm_w.reshape(SSD_GROUPS, SSD_INNER // SSD_GROUPS)).reshape(B, T, SSD_INNER)
    q = rope_partial(rmsnorm(q.reshape(B, T, ATT_HEADS, HEAD_DIM), q_norm_w), pos) * (HEAD_DIM ** -0.5)
    k = rope_partial(rmsnorm(k.reshape(B, T, ATT_KV_HEADS, HEAD_DIM), k_norm_w), pos)
    v = v.reshape(B, T, ATT_KV_HEADS, HEAD_DIM)
    o, k_state, v_state = attend(q.reshape(B, T, ATT_KV_HEADS, ATT_GQ, HEAD_DIM), k, v)
    y_att = o.reshape(B, T, ATT_INNER) * jax.nn.silu(gate)
    out = jnp.concatenate([y_ssd, y_att], axis=-1) @ w_out.astype(jnp.float32)
    return (x + out).astype(x.dtype), conv_new, h_new, k_state, v_state


def setup_inputs(seed: int = 0) -> dict:
    key = jax.random.key(seed)
    ks = jax.random.split(key, 20)
    win = min(W_MAX, PAST_LEN)
    f32 = jnp.float32
    dt0 = jnp.exp(jax.random.uniform(ks[10], (DEPTH, SSD_HEADS), f32)
                  * (np.log(0.1) - np.log(0.001)) + np.log(0.001))
    return {
        "x_prompt": jax.random.normal(ks[0], (BATCH, SEQ, D_MODEL), f32),
        "x_sample": jax.random.normal(ks[1], (DEC_BATCH, DEC_SEQ, D_MODEL), f32),
        "cache_k": jax.random.normal(ks[2], (DEPTH, DEC_BATCH, win, ATT_KV_HEADS, HEAD_DIM), f32),
        "cache_v": jax.random.normal(ks[3], (DEPTH, DEC_BATCH, win, ATT_KV_HEADS, HEAD_DIM), f32),
        "state_conv": jax.random.normal(ks[4], (DEPTH, DEC_BATCH, CONV_W - 1, CONV_DIM), f32),
        "state_ssm": 0.5 * jax.random.normal(ks[5], (DEPTH, DEC_BATCH, SSD_HEADS, SSD_HEADDIM, SSD_STATE), f32),
        "norm_w": 1.0 + 0.01 * jax.random.normal(ks[6], (DEPTH, D_MODEL), f32),
        "w_in": jax.random.normal(ks[7], (DEPTH, D_MODEL, IN_COLS), f32) * D_MODEL ** -0.5,
        "conv_w": jax.random.normal(ks[8], (DEPTH, CONV_W, CONV_DIM), f32) * CONV_W ** -0.5,
        "conv_b": 0.01 * jax.random.normal(ks[9], (DEPTH, CONV_DIM), f32),
        "dt_bias": dt0 + jnp.log(-jnp.expm1(-dt0)),
        "a_log": jnp.log(jax.random.uniform(ks[11], (DEPTH, SSD_HEADS), f32, 1.0, 16.0)),
        "d_skip": 1.0 + 0.1 * jax.random.normal(ks[12], (DEPTH, SSD_HEADS), f32),
        "ssd_norm_w": 1.0 + 0.01 * jax.random.normal(ks[13], (DEPTH, SSD_INNER), f32),
        "q_norm_w": 1.0 + 0.01 * jax.random.normal(ks[14], (DEPTH, HEAD_DIM), f32),
        "k_norm_w": 1.0 + 0.01 * jax.random.normal(ks[15], (DEPTH, HEAD_DIM), f32),
        "w_out": jax.random.normal(ks[16], (DEPTH, D_MIX, D_MODEL), f32) * D_MIX ** -0.5,
    }


def reference(x_prompt, x_sample, cache_k, cache_v, state_conv, state_ssm, norm_w, w_in, conv_w,
              conv_b, dt_bias, a_log, d_skip, ssd_norm_w, q_norm_w, k_norm_w, w_out):
    Bp, Tp = x_prompt.shape[0], x_prompt.shape[1]
    Bs, Ts = x_sample.shape[0], x_sample.shape[1]
    pos_p = jnp.arange(Tp, dtype=jnp.int32)
    pos_s = PAST_LEN + jnp.arange(Ts, dtype=jnp.int32)
    yp, ys = x_prompt, x_sample
    kp, vp, cp, hp, kss, vss, css, hss = [], [], [], [], [], [], [], []
    for l in range(DEPTH):
        weights = (norm_w[l], w_in[l], conv_w[l], conv_b[l], dt_bias[l], a_log[l], d_skip[l],
                   ssd_norm_w[l], q_norm_w[l], k_norm_w[l], w_out[l])
        yp, c_new, h_new, k_new, v_new = hybrid_layer(
            yp, pos_p, jnp.zeros((Bp, CONV_W - 1, CONV_DIM), jnp.float32),
            jnp.zeros((Bp, SSD_HEADS, SSD_HEADDIM, SSD_STATE), jnp.float32),
            min(SSD_CHUNK, Tp), prompt_attention, *weights)
        kp.append(k_new); vp.append(v_new); cp.append(c_new); hp.append(h_new)
        ck, cv = cache_k[l], cache_v[l]
        ys, c_new, h_new, k_new, v_new = hybrid_layer(
            ys, pos_s, state_conv[l], state_ssm[l], Ts,
            lambda q, k, v: sample_attention(q, k, v, ck, cv), *weights)
        kss.append(k_new); vss.append(v_new); css.append(c_new); hss.append(h_new)
    return (yp, ys, jnp.stack(kp), jnp.stack(vp), jnp.stack(cp), jnp.stack(hp),
            jnp.stack(kss), jnp.stack(vss), jnp.stack(css), jnp.stack(hss))
```

```python
import numpy as np
import ml_dtypes
from contextlib import ExitStack
import concourse.bass as bass
import concourse.mybir as mybir
from concourse.bass_utils import run_bass_kernel_spmd

F32 = mybir.dt.float32
BF16 = mybir.dt.bfloat16
AF = mybir.ActivationFunctionType
ALU = mybir.AluOpType
AX = mybir.AxisListType

D = 2048
NTOK = 2052
EPS = 1e-6
C_Z, C_XBC, C_DT, C_Q, C_K, C_V, C_G = 0, 1024, 2560, 2576, 3600, 3856, 4112
IN_COLS = 5136
NEG = -30000.0


class Trk:
    def __init__(self, nc, es):
        self.nc = nc
        self.es = es
        self.eng = {'pe': nc.tensor, 'act': nc.scalar, 'dve': nc.vector, 'pool': nc.gpsimd, 'sp': nc.sync}
        self.sems = {}
        self.waited = {e: {} for e in self.eng}
        self.last_w = {}
        self.readers = {}
        self.pending = {e: [] for e in self.eng}
        self.nins = 0
        self.excl = set()

    def sem(self, name):
        if name not in self.sems:
            self.sems[name] = [self.es.enter_context(self.nc.semaphore(name)), 0]
        return self.sems[name]

    def _wait(self, e, tok):
        if tok is None:
            return
        name, val = tok
        if e == 'pe' and name == 'S_pe':
            return
        if self.waited[e].get(name, 0) >= val:
            return
        self.eng[e].wait_ge(self.sems[name][0], val)
        self.waited[e][name] = val
        self.nins += 1

    def _chk_pending(self, e, region):
        for e2, lst in self.pending.items():
            if e2 == e:
                continue
            for (_, r) in lst:
                if r == region:
                    raise RuntimeError(f"region {region} has pending un-signalled op on {e2}")

    def op(self, e, fn, reads=(), writes=(), inc=True, semname=None, incval=1):
        need = {}

        def want(tok, skip_own=False):
            if tok is None:
                return
            n, v = tok
            if e == 'pe' and n == 'S_pe':
                return
            if skip_own and n == 'S_' + e:
                return
            if need.get(n, 0) < v:
                need[n] = v

        xr = [r for r in reads if r in self.excl]
        for r in xr:
            self._chk_pending(e, r)
            want(self.last_w.get(r), True)
            for t in self.readers.get(r, []):
                want(t, True)
        reads = [r for r in reads if r not in self.excl]
        for r in reads:
            self._chk_pending(e, r)
            want(self.last_w.get(r))
        for w in writes:
            self._chk_pending(e, w)
            want(self.last_w.get(w))
            for t in self.readers.get(w, []):
                want(t)
        for n, v in need.items():
            self._wait(e, (n, v))
        writes = list(writes) + xr
        ins = fn()
        self.nins += 1
        if semname is None:
            for r in reads:
                self.pending[e].append(('r', r))
            for w in writes:
                self.pending[e].append(('w', w))
            if inc:
                s = self.sem('S_' + e)
                s[1] += 1
                ins.then_inc(s[0], 1)
                tok = ('S_' + e, s[1])
                for kind, r in self.pending[e]:
                    if kind == 'r':
                        self.readers.setdefault(r, []).append(tok)
                    else:
                        self.last_w[r] = tok
                        self.readers[r] = []
                self.pending[e] = []
        else:
            s = self.sem(semname)
            s[1] += incval
            ins.then_inc(s[0], incval)
            tok = (semname, s[1])
            for r in reads:
                self.readers.setdefault(r, []).append(tok)
            for w in writes:
                self.last_w[w] = tok
                self.readers[w] = []
        return ins

    def wait_free(self, e, keys):
        for k in keys:
            self._wait(e, self.last_w.get(k))
            for t in self.readers.get(k, []):
                self._wait(e, t)

    def barrier(self):
        for e, lst in self.pending.items():
            assert not lst, (e, lst)
        for name, (h, cnt) in list(self.sems.items()):
            if cnt == 0:
                continue
            for e in self.eng:
                self._wait(e, (name, cnt))

    def final_wait(self, e='sp'):
        for name, (h, cnt) in list(self.sems.items()):
            if cnt:
                self._wait(e, (name, cnt))


def build_nc():
    nc = bass.Bass("TRN2", target_bir_lowering=False)
    dram = {}

    def din(name, shape, dt=F32):
        dram[name] = nc.dram_tensor(name, list(shape), dt, kind="ExternalInput").ap()
        return dram[name]

    def dout(name, shape, dt=F32):
        dram[name] = nc.dram_tensor(name, list(shape), dt, kind="ExternalOutput").ap()
        return dram[name]

    xh = din("xh", [1024, D]); xo = din("xo", [1024, D]); xsm = din("xsm", [4, D])
    w_in = din("w_in", [D, IN_COLS]); w_out = din("w_out", [D, D])
    cache_k = din("cache_k", [2048, 256]); cache_v = din("cache_v", [2048, 256])
    sc_fm_d = din("sc_fm", [128, 12, 3]); st_ssm = din("st_ssm", [1024, 128])
    normw_d = din("normw_bc", [128, D]); qnw_d = din("qnw_bc", [128, 64]); knw_d = din("knw_bc", [128, 64])
    cos_d = din("c_cos", [128, 17, 8]); sin_d = din("c_sin", [128, 17, 8])
    dtb_d = din("dtb_bc", [128, 16]); alog_d = din("alog_bc", [128, 16])
    dskip_d = din("dskip_bc", [128, 1024]); ssdnw_d = din("ssdnw_bc", [128, 1024])
    convw_d = din("convw_fm", [128, 12, 4]); convb_d = din("convb_fm", [128, 12])
    pm_d = din("pm", [128, 17, 128], BF16); sm_d = din("smk", [128, 17, 4], BF16)
    flags_d = din("flags", [128, 4])
    identb_d = din("identb", [128, 128], BF16); identf_d = din("identf", [128, 128])
    tri_d = din("tri", [128, 128]); g_d = din("gmat", [128, 128])

    y_o = dout("y_o", [1024, D]); y_s = dout("y_s", [4, D])
    k_o = dout("k_o", [1024, 256]); v_o = dout("v_o", [1024, 256])
    conv_o = dout("conv_o", [3, 1536]); ssm_o = dout("ssm_o", [1024, 128])
    ks_o = dout("ks_o", [2048, 256]); vs_o = dout("vs_o", [2048, 256])
    convs_o = dout("convs_o", [3, 1536]); ssms_o = dout("ssms_o", [1024, 128])

    with ExitStack() as es:
        T = Trk(nc, es)

        def sb(name, shape, dt=F32, stack=es):
            return stack.enter_context(nc.sbuf_tensor("s_" + name, list(shape), dt))

        def ps(name, shape, dt=F32, stack=es):
            return stack.enter_context(nc.psum_tensor("p_" + name, list(shape), dt))

        def dma(q, out, in_, reads, writes, sem, slow=False):
            eng = T.eng[q]
            if slow:
                f = lambda: eng.dma_start(out=out, in_=in_, allow_slow_non_contiguous=True)
            else:
                f = lambda: eng.dma_start(out=out, in_=in_)
            return T.op(q, f, reads, writes, semname=sem, incval=16)

        def mm(out, lhsT, rhs, start, stop, reads, writes, last):
            return T.op('pe', lambda: nc.tensor.matmul(out, lhsT=lhsT, rhs=rhs, start=start, stop=stop),
                        reads, writes, inc=last)

        def tr(out, in_, ident, reads, writes, last):
            return T.op('pe', lambda: nc.tensor.transpose(out=out, in_=in_, identity=ident),
                        reads, writes, inc=last)

        def act(out, in_, func, reads, writes, **kw):
            return T.op('act', lambda: nc.scalar.activation(out=out, in_=in_, func=func, **kw), reads, writes)

        def vtt(out, in0, in1, op, reads, writes, e='dve'):
            return T.op(e, lambda: T.eng[e].tensor_tensor(out=out, in0=in0, in1=in1, op=op), reads, writes)

        def vts(out, in0, s1, s2, op0, op1, reads, writes, e='dve'):
            if op1 is None:
                return T.op(e, lambda: T.eng[e].tensor_scalar(out=out, in0=in0, scalar1=s1, scalar2=None, op0=op0),
                            reads, writes)
            return T.op(e, lambda: T.eng[e].tensor_scalar(out=out, in0=in0, scalar1=s1, scalar2=s2, op0=op0, op1=op1),
                        reads, writes)

        def vstt(out, in0, scalar, in1, op0, op1, reads, writes, e='dve'):
            return T.op(e, lambda: T.eng[e].scalar_tensor_tensor(out=out, in0=in0, scalar=scalar, in1=in1,
                                                                  op0=op0, op1=op1), reads, writes)

        def vcopy(out, in_, reads, writes, e='dve'):
            return T.op(e, lambda: T.eng[e].tensor_copy(out=out, in_=in_), reads, writes)

        def vmemset(ap, val, writes, e='dve'):
            return T.op(e, lambda: T.eng[e].memset(ap, val), (), writes)

        def vreduce(out, in_, reads, writes):
            return T.op('dve', lambda: nc.vector.tensor_reduce(out=out, in_=in_, axis=AX.X, op=ALU.add), reads, writes)

        def rsqrt_chain(dst, ss, n, L, scale, reads_key, tmpa, tmpb):
            vts(tmpa[0:L, 0:n], ss, scale, EPS, ALU.mult, ALU.add, [reads_key], ['rs_a'])
            act(tmpb[0:L, 0:n], tmpa[0:L, 0:n], AF.Ln, ['rs_a'], ['rs_b'])
            act(dst, tmpb[0:L, 0:n], AF.Exp, ['rs_b'], [reads_key + '_r'], scale=-0.5)

        identb = sb("identb", [128, 128], BF16); identf = sb("identf", [128, 128])
        tri = sb("tri", [128, 128]); gmat = sb("gmat", [128, 128]); ones = sb("ones", [128, 128])
        qnw = sb("qnw", [128, 64]); knw = sb("knw", [128, 64])
        ccos = sb("ccos", [128, 17, 8]); csin = sb("csin", [128, 17, 8])
        dtb = sb("dtb", [128, 16]); abc = sb("abc", [128, 16])
        convw = sb("convw", [128, 12, 4]); convb = sb("convb", [128, 12])
        smk = sb("smk", [128, 17, 4], BF16)
        flags = sb("flags", [128, 4]); zcol = sb("zcol", [128, 1])
        scfm = sb("scfm", [128, 12, 3])
        rs_a = sb("rs_a", [128, 16]); rs_b = sb("rs_b", [128, 16])
        y_mix = sb("y_mix", [128, 16, 1028], BF16)
        wraw = [sb("wraw0", [128, 2048], F32), sb("wraw1", [128, 2048], F32)]
        wst = [w_[:].bitcast(BF16).rearrange("p (k n) -> p k n", k=16) for w_ in wraw]
        dt_tm = sb("dt_tm", [128, 17, 16])

        consts = [(identb, identb_d, 'identb'), (identf, identf_d, 'identf'), (tri, tri_d, 'tri'),
                  (gmat, g_d, 'gmat'), (qnw, qnw_d, 'qnw'), (knw, knw_d, 'knw'), (ccos, cos_d, 'rope'),
                  (csin, sin_d, 'rope2'), (dtb, dtb_d, 'dtb'), (abc, alog_d, 'abc'),
                  (convw, convw_d, 'convw'), (convb, convb_d, 'convb'),
                  (smk, sm_d, 'smk'), (flags, flags_d, 'flags'), (scfm, sc_fm_d, 'scfm')]
        for t_, d_, key in consts:
            dma('sp', t_[:], d_, [], [key], 'd_const')
        vmemset(ones[:], 1.0, ['ones'])
        vmemset(zcol[:], 0.0, ['zcol'])
        act(abc[:], abc[:], AF.Exp, ['abc'], ['abc'])
        vts(abc[:], abc[:], -1.0, None, ALU.mult, None, ['abc'], ['abc'])

        wstate = {'n': 0}

        wmode = {}

        def load_w(src, c0, ncols, blocks=None):
            b = wstate['n'] % 2
            wstate['n'] += 1
            allk = [('wst', b, i) for i in range(4)]
            if blocks is None:
                wmode[b] = 'kc'
                T.wait_free('pool', allk)
                v = src[:, c0:c0 + ncols].rearrange("(kc p) n -> p kc n", p=128)
                for qd in range(4):
                    dma('pool', wst[b][:, qd * 4:(qd + 1) * 4, 0:ncols], v[:, qd * 4:(qd + 1) * 4, :],
                        [], [('wst', b, qd)], 'd_w%d_%d' % (b, qd))
            else:
                wmode[b] = 'col'
                T.wait_free('pool', allk)
                for i, cb in enumerate(blocks):
                    v = src[:, cb:cb + 64].rearrange("(kc p) n -> p kc n", p=128)
                    dma('pool', wst[b][:, :, i * 64:(i + 1) * 64], v, [], [('wst', b, i)], 'd_w%d_%d' % (b, i))
            return b

        def wkeys(b, kc):
            if wmode[b] == 'kc':
                return [('wst', b, kc // 4)]
            return [('wst', b, i) for i in range(4)]

        sched = []
        sched.append(('k', w_in, C_K, 256, None))
        sched.append(('v', w_in, C_V, 256, None))
        sched.append(('dt', w_in, C_DT, 16, None))
        for i in range(6):
            sched.append(('x%d' % i, w_in, C_XBC + 256 * i, 256, None))
        for u in range(4):
            m_, jj0 = u // 2, 2 * (u % 2)
            hs = [8 * m_ + jj0, 8 * m_ + 4 + jj0, 8 * m_ + jj0 + 1, 8 * m_ + 4 + jj0 + 1]
            sched.append(('q%d' % u, w_in, None, 256, [C_Q + 64 * h_ for h_ in hs]))
        for i in range(4):
            sched.append(('g%d' % i, w_in, C_G + 256 * i, 256, None))
        for i in range(4):
            sched.append(('z%d' % i, w_in, C_Z + 256 * i, 256, None))
        sched.append(('oa', w_out, 0, 256, None))
        sched.append(('ob', w_out, 256, 256, None))
        wq = {'i': 0, 'buf': {}}

        def w_issue():
            i = wq['i']
            if i < len(sched):
                name, src, c0, ncols, blocks = sched[i]
                wq['buf'][name] = load_w(src, c0, ncols, blocks)
                wq['i'] += 1

        def w_get(name):
            while name not in wq['buf']:
                w_issue()
            w_issue()
            return wq['buf'][name]

        psA = ExitStack()
        ptp = [ps("ptp0", [128, 2048], BF16, psA), ps("ptp1", [128, 2048], BF16, psA)]
        T.excl.update([('ptp', 0, 0), ('ptp', 0, 1), ('ptp', 1, 0), ('ptp', 1, 1)])

        s1 = es.enter_context(ExitStack())
        xbc_h = sb("xbc_h", [128, 10, 1024], BF16, s1)
        xbc_o = sb("xbc_o", [128, 12, 1028], BF16, s1)
        K_FM = sb("K_FM", [128, 2, NTOK], BF16, s1)
        V_TM = sb("V_TM", [128, 17, 4, 65], BF16, s1)
        s2 = es.enter_context(ExitStack())
        hnT_o = sb("hnT_o", [128, 16, 1028], BF16, s2)
        s3 = ExitStack()
        hnT_h = sb("hnT_h", [128, 16, 1024], BF16, s3)

        def hn(kc, bi, L):
            if bi < 8:
                return hnT_h[:, kc, bi * 128:bi * 128 + L]
            return hnT_o[:, kc, (bi - 8) * 128:(bi - 8) * 128 + L]

        def xbc_ap(j, bi, L):
            if bi < 8:
                jj = j if j < 8 else j
                return xbc_h[:, jj, bi * 128:bi * 128 + L]
            return xbc_o[:, j, (bi - 8) * 128:(bi - 8) * 128 + L]

        with ExitStack() as pa:
            normw = sb("normw", [128, D], F32, pa)
            xt = [sb("xt0", [128, D], F32, pa), wraw[0], wraw[1]]
            xn = [sb("xn0", [128, D], BF16, pa), sb("xn1", [128, D], BF16, pa)]
            ssA = [sb("ssA%d" % i, [128, 1], F32, pa) for i in range(2)]
            rstdA = [sb("rstdA%d" % i, [128, 1], F32, pa) for i in range(2)]
            rsA = [sb("rsA%d" % i, [128, 1], F32, pa) for i in range(2)]
            rsB = [sb("rsB%d" % i, [128, 1], F32, pa) for i in range(2)]
            dma('sp', normw[:], normw_d, [], ['normw'], 'd_const')

            def a_stage1(bi):
                L = 128 if bi < 16 else 4
                if bi < 8:
                    src = xh[bi * 128:(bi + 1) * 128, :]
                elif bi < 16:
                    src = xo[(bi - 8) * 128:(bi - 7) * 128, :]
                else:
                    src = xsm
                b = bi % 2
                xb = bi % 3
                dma('sp' if bi % 2 == 0 else 'pool', xt[xb][0:L, :], src, [], [('xt', xb)], 'd_x%d' % xb)
                act(xn[b][0:L, :], xt[xb][0:L, :], AF.Square, [('xt', xb), ('ssA', b)], [('xn', b), ('ssA', b)],
                    accum_out=ssA[b][0:L, :])
                vts(rsA[b][0:L, :], ssA[b][0:L, :], 1.0 / D, EPS, ALU.mult, ALU.add, [('ssA', b)], [('rsA', b)])
                vmemset(ssA[b][:, :], 0.0, [('ssA', b)])
                act(rsB[b][0:L, :], rsA[b][0:L, :], AF.Ln, [('rsA', b)], [('rsB', b)])
                act(rstdA[b][0:L, :], rsB[b][0:L, :], AF.Exp, [('rsB', b)], [('rstdA', b)], scale=-0.5)
                vstt(xn[b][0:L, :], xt[xb][0:L, :], rstdA[b][0:L, 0:1], normw[0:L, :], ALU.mult, ALU.mult,
                     [('xt', xb), ('rstdA', b), 'normw'], [('xn', b)])

            def a_stage2(bi):
                L = 128 if bi < 16 else 4
                b = bi % 2
                for kc in range(16):
                    tr(ptp[b][:, kc * 128:kc * 128 + L], xn[b][0:L, kc * 128:(kc + 1) * 128], identb[0:L, 0:L],
                       [('xn', b), 'identb'], [('ptp', b, kc // 8)], last=(kc % 8 == 7))
                v3 = ptp[b][:].rearrange("p (k t) -> p k t", k=16)
                if bi < 8:
                    d0 = hnT_h[:, 0:8, bi * 128:bi * 128 + L]; d1 = hnT_h[:, 8:16, bi * 128:bi * 128 + L]
                else:
                    o_ = (bi - 8) * 128
                    d0 = hnT_o[:, 0:8, o_:o_ + L]; d1 = hnT_o[:, 8:16, o_:o_ + L]
                act(d0, v3[:, 0:8, 0:L], AF.Copy, [('ptp', b, 0)], [('hnT', bi)])
                vcopy(d1, v3[:, 8:16, 0:L], [('ptp', b, 1)], [('hnT', bi)])

            for b_ in range(2):
                vmemset(ssA[b_][:, :], 0.0, [('ssA', b_)])
            a_stage1(0)
            for bi in range(17):
                if bi + 1 < 17:
                    a_stage1(bi + 1)
                a_stage2(bi)
        T.barrier()
        psA.close()

        pmm = [ps("pmm0", [128, 512]), ps("pmm1", [128, 512])]
        ptr = ps("ptr", [128, 1024], BF16)
        pss = [ps("pss0", [128, 512]), ps("pss1", [128, 512])]
        psob = [ps("pso0", [128, 512]), ps("pso1", [128, 512])]
        pso = [p_[:, 0:260].rearrange("p (j d) -> p j d", j=4) for p_ in psob]
        pY = pss[0]
        pst = pss[1]
        psmall = ps("psmall", [128, 512])
        psm = psmall[:, 0:16]
        pcb = pmm[0][:, 0:128]
        pseg = [psob[0][:, 0:128], psob[1][:, 0:128]]
        PSM, PCB, PSEG = 'psmall', ('pmm', 0), [('pso', 0), ('pso', 1)]
        T.excl.update([('pmm', 0), ('pmm', 1), 'ptr', ('pss', 0), ('pss', 1), ('pso', 0), ('pso', 1), 'psmall'])

        mmcnt = {'n': 0}

        def proj_tm(wb, ncols, bi):
            L = 128 if bi < 16 else 4
            pb = mmcnt['n'] % 2
            mmcnt['n'] += 1
            for kc in range(16):
                mm(pmm[pb][0:L, 0:ncols], hn(kc, bi, L), wst[wb][:, kc, 0:ncols],
                   kc == 0, kc == 15, [('hnT', bi)] + wkeys(wb, kc), [('pmm', pb)], last=(kc == 15))
            return pb, L

        def run_rr(gens):
            gens = list(gens)
            while gens:
                for g_ in list(gens):
                    try:
                        next(g_)
                    except StopIteration:
                        gens.remove(g_)

        def make_norm_rope(stack, tag):
            tm = []
            for par in range(2):
                t_ = {k_: sb("%s_%s%d" % (tag, k_, par), shp, dt_, stack) for k_, shp, dt_ in [
                    ('xw', [128, 256], F32), ('ra', [128, 4, 8], F32), ('rb', [128, 4, 8], F32),
                    ('rc', [128, 4, 8], F32), ('rd', [128, 4, 8], F32), ('ssq', [128, 4], F32),
                    ('rq', [128, 4], F32), ('junk', [128, 64], BF16), ('rsa', [128, 4], F32), ('rsb', [128, 4], F32)]}
                tm.append(t_)
                vmemset(t_['ssq'][:, :], 0.0, [(tag, 'ssq', par)])
            cntr = {'n': 0}

            def norm_rope(pb, L, bi, wbc, wkey, out3, out_key):
                par = cntr['n'] % 2
                cntr['n'] += 1
                yield
                t_ = tm[par]
                K = lambda k_: (tag, k_, par)
                nh = 4
                for h_ in range(nh):
                    act(t_['junk'][0:L, :], pmm[pb][0:L, h_ * 64:(h_ + 1) * 64], AF.Square, [('pmm', pb), K('ssq')],
                        [K('junk'), K('ssq')], accum_out=t_['ssq'][0:L, h_:h_ + 1])
                yield
                x3 = t_['xw'][0:L, :].rearrange("p (h d) -> p h d", h=nh)
                vtt(x3, pmm[pb][0:L, 0:256].rearrange("p (h d) -> p h d", h=nh),
                    wbc[0:L, :].unsqueeze(1).broadcast_to([L, nh, 64]), ALU.mult, [('pmm', pb), wkey], [K('xw')])
                yield
                cosb = ccos[0:L, bi, :].unsqueeze(1).broadcast_to([L, nh, 8])
                sinb = csin[0:L, bi, :].unsqueeze(1).broadcast_to([L, nh, 8])
                vtt(t_['ra'][0:L], x3[:, :, 0:8], cosb, ALU.mult, [K('xw'), 'rope'], [K('ra')])
                vtt(t_['rb'][0:L], x3[:, :, 8:16], sinb, ALU.mult, [K('xw'), 'rope2'], [K('rb')])
                yield
                vtt(t_['rc'][0:L], x3[:, :, 8:16], cosb, ALU.mult, [K('xw'), 'rope'], [K('rc')])
                vtt(t_['rd'][0:L], x3[:, :, 0:8], sinb, ALU.mult, [K('xw'), 'rope2'], [K('rd')])
                yield
                vts(t_['rsa'][0:L, :], t_['ssq'][0:L, :], 1.0 / 64, EPS, ALU.mult, ALU.add, [K('ssq')], [K('rsa')])
                vmemset(t_['ssq'][:, :], 0.0, [K('ssq')])
                yield
                act(t_['rsb'][0:L, :], t_['rsa'][0:L, :], AF.Ln, [K('rsa')], [K('rsb')])
                act(t_['rq'][0:L, :], t_['rsb'][0:L, :], AF.Exp, [K('rsb')], [K('rq')], scale=-0.5)
                yield
                vtt(x3[:, :, 0:8], t_['ra'][0:L], t_['rb'][0:L], ALU.subtract, [K('ra'), K('rb')], [K('xw')])
                yield
                vtt(x3[:, :, 8:16], t_['rc'][0:L], t_['rd'][0:L], ALU.add, [K('rc'), K('rd')], [K('xw')])
                vtt(out3, x3, t_['rq'][0:L, :].unsqueeze(2).broadcast_to([L, nh, 64]), ALU.mult,
                    [K('xw'), K('rq')], [out_key])
                yield
            return norm_rope

        own_blocks = list(range(8, 16)) + [16]

        with ExitStack() as pb1:
            fin = sb("fin", [128, 256], F32, pb1)
            fin16 = sb("fin16", [128, 256], BF16, pb1); vt = sb("vt", [128, 256], F32, pb1)
            rawc = [sb("rawc0", [128, 3 + 2048], F32, pb1), sb("rawc1", [128, 3 + 2048], F32, pb1)]
            raws = sb("raws", [128, 8], F32, pb1)
            cva = sb("cva", [128, 1024], F32, pb1)
            cvs = sb("cvs", [128, 4], F32, pb1)
            nr_k = make_norm_rope(pb1, 'nrk')

            vmemset(V_TM[:].rearrange("p a b c -> p (a b c)"), 1.0, [('V_TM', i_) for i_ in range(17)])
            dma('sp', ks_o[0:2044, :], cache_k[4:2048, :], [], ['ks_o'], 'd_out')
            dma('sp', vs_o[0:2044, :], cache_v[4:2048, :], [], ['vs_o'], 'd_out')
            wb = w_get('k')
            fin2 = [fin, sb("finb", [128, 256], F32, pb1)]
            fin162 = [fin16, sb("fin16b", [128, 256], BF16, pb1)]

            def finish_k(bi, L, par):
                for m in range(2):
                    tr(ptr[:, m * 128:m * 128 + L], fin162[par][0:L, m * 128:(m + 1) * 128], identb[0:L, 0:L],
                       [('fin16', par), 'identb'], ['ptr'], last=(m == 1))
                v3 = ptr[:, 0:256].rearrange("p (m t) -> p m t", m=2)
                act(K_FM[:, :, bi * 128:bi * 128 + L], v3[:, :, 0:L], AF.Copy, ['ptr'], ['K_FM'])

            def k_chain(pb, L, bi, par):
                yield from nr_k(pb, L, bi, knw, 'knw', fin2[par][0:L, :].rearrange("p (h d) -> p h d", h=4), ('fin', par))
                if 8 <= bi < 16:
                    dma('sp', k_o[(bi - 8) * 128:(bi - 7) * 128, :], fin2[par][0:L, :], [('fin', par)], ['k_o'], 'do_fin%d' % par)
                elif bi == 16:
                    dma('sp', ks_o[2044:2048, :], fin2[par][0:L, :], [('fin', par)], ['ks_o'], 'do_fin%d' % par)
                vcopy(fin162[par][0:L, :], fin2[par][0:L, :], [('fin', par)], [('fin16', par)])
                yield

            prevs = []
            kblocks = list(range(17))
            for i0 in range(0, 17, 2):
                pair = kblocks[i0:i0 + 2]
                cur = []
                for j_, bi in enumerate(pair):
                    pb, L = proj_tm(wb, 256, bi)
                    cur.append((pb, L, bi, j_))
                for p_ in prevs:
                    finish_k(*p_)
                run_rr([k_chain(pb, L, bi, par) for (pb, L, bi, par) in cur])
                prevs = [(bi, L, par) for (pb, L, bi, par) in cur]
            for p_ in prevs:
                finish_k(*p_)
            wb = w_get('v')
            vt2 = [vt, fin2[1]]
            for bi in range(17):
                pb, L = proj_tm(wb, 256, bi)
                vb = bi % 2
                vkey = 'vt' if vb == 0 else ('fin', 1)
                act(vt2[vb][0:L, :], pmm[pb][0:L, 0:256], AF.Copy, [('pmm', pb)], [vkey])
                if 8 <= bi < 16:
                    dma('sp', v_o[(bi - 8) * 128:(bi - 7) * 128, :], vt2[vb][0:L, :], [vkey], ['v_o'], 'do_vt%d' % vb)
                elif bi == 16:
                    dma('sp', vs_o[2044:2048, :], vt2[vb][0:L, :], [vkey], ['vs_o'], 'do_vt%d' % vb)
                vcopy(V_TM[0:L, bi, :, 0:64], vt2[vb][0:L, :].rearrange("p (g d) -> p g d", g=4), [vkey], [('V_TM', bi)])
            wb = w_get('dt')
            for bi in range(17):
                pb, L = proj_tm(wb, 16, bi)
                vtt(dt_tm[0:L, bi, :], pmm[pb][0:L, 0:16], dtb[0:L, :], ALU.add, [('pmm', pb), 'dtb'], ['dt_tm'])
                act(dt_tm[0:L, bi, :], dt_tm[0:L, bi, :], AF.Exp, ['dt_tm'], ['dt_tm'])
                act(dt_tm[0:L, bi, :], dt_tm[0:L, bi, :], AF.Ln, ['dt_tm'], ['dt_tm'], bias=1.0)
            def xbc_post(j):
                rb = j % 2
                dma('sp', conv_o.rearrange("k (t p) -> p t k", p=128)[:, j, :], rawc[rb][:, 2048:2051],
                    [('rawc', rb)], ['conv_o'], 'do_rawc%d' % rb, slow=True)
                dma('sp', convs_o.rearrange("k (t p) -> p t k", p=128)[:, j, :], raws[rb][:, 4:7],
                    [('raws', rb)], ['convs_o'], 'do_raws%d' % rb, slow=True)
                for hf in range(2):
                    if hf == 0 and j >= 10:
                        continue
                    o_ = hf * 1024
                    vts(cva[:, :], rawc[rb][:, o_ + 3:o_ + 3 + 1024], convw[:, j, 3:4], convb[:, j:j + 1], ALU.mult, ALU.add,
                        [('rawc', rb), 'convw', 'convb'], ['cva'])
                    for k in (2, 1, 0):
                        vstt(cva[:, :], rawc[rb][:, o_ + k:o_ + k + 1024], convw[:, j, k:k + 1], cva[:, :], ALU.mult, ALU.add,
                             [('rawc', rb), 'convw', 'cva'], ['cva'])
                    if hf == 0:
                        act(xbc_h[:, j, :], cva[:, :], AF.Silu, ['cva'], ['xbc'])
                    else:
                        act(xbc_o[:, j, 0:1024], cva[:, :], AF.Silu, ['cva'], ['xbc'])
                vts(cvs[:, :], raws[rb][:, 3:7], convw[:, j, 3:4], convb[:, j:j + 1], ALU.mult, ALU.add,
                    [('raws', rb), 'convw', 'convb'], ['cvs'])
                for k in (2, 1, 0):
                    vstt(cvs[:, :], raws[rb][:, k:k + 4], convw[:, j, k:k + 1], cvs[:, :], ALU.mult, ALU.add,
                         [('raws', rb), 'convw', 'cvs'], ['cvs'])
                act(xbc_o[:, j, 1024:1028], cvs[:, :], AF.Silu, ['cvs'], ['xbc'])

            raws = [raws, sb("raws_b", [128, 8], F32, pb1)]
            pending_j = None
            for gx in range(6):
                wb = w_get('x%d' % gx)
                for jj in range(2):
                    j = gx * 2 + jj
                    rb = j % 2
                    vmemset(rawc[rb][:, 0:3], 0.0, [('rawc', rb)])
                    for tt in range(4):
                        pb = mmcnt['n'] % 2
                        mmcnt['n'] += 1
                        for kc in range(16):
                            rhs = hnT_h[:, kc, tt * 512:(tt + 1) * 512] if tt < 2 else hnT_o[:, kc, (tt - 2) * 512:(tt - 1) * 512]
                            mm(pmm[pb][:, :], wst[wb][:, kc, jj * 128:(jj + 1) * 128], rhs,
                               kc == 0, kc == 15, [('hnT', tt * 4 + q_) for q_ in range(4)] + wkeys(wb, kc),
                               [('pmm', pb)], last=(kc == 15))
                        act(rawc[rb][:, 3 + tt * 512:3 + (tt + 1) * 512], pmm[pb][:, :], AF.Copy, [('pmm', pb)], [('rawc', rb)])
                    pb = mmcnt['n'] % 2
                    mmcnt['n'] += 1
                    for kc in range(16):
                        mm(pmm[pb][:, 0:4], wst[wb][:, kc, jj * 128:(jj + 1) * 128], hnT_o[:, kc, 1024:1028],
                           kc == 0, kc == 15, [('hnT', 16)] + wkeys(wb, kc), [('pmm', pb)], last=(kc == 15))
                    vcopy(raws[rb][:, 0:3], scfm[:, j, :], ['scfm'], [('raws', rb)])
                    act(raws[rb][:, 3:7], pmm[pb][:, 0:4], AF.Copy, [('pmm', pb)], [('raws', rb)])
                    if pending_j is not None:
                        xbc_post(pending_j)
                    pending_j = j
            xbc_post(pending_j)
        T.barrier()
        s3.close()

        with ExitStack() as pc:
            Q_FM = sb("Q_FM", [128, 8, 1028], BF16, pc)
            sg = sb("sg", [128, 9, 1024], BF16, pc)
            pm = sb("pm", [128, 17, 128], BF16, pc)
            raw = sb("raw2", [128, 256], F32, pc)
            fin16 = sb("fin162", [128, 256], BF16, pc)
            nr_q = make_norm_rope(pc, 'nrq')
            pt = [sb("pt%d" % i, [128, 512], BF16, pc) for i in range(6)]
            pp = [sb("pp%d" % i, [128, 512], BF16, pc) for i in range(6)]
            pss3 = [(pss[0], ('pss', 0)), (pss[1], ('pss', 1)), (pmm[0], ('pmm', 0)), (pmm[1], ('pmm', 1)),
                    (psmall, 'psmall'), (psob[1], ('pso', 1))]
            rden = sb("rden", [128, 4], F32, pc)
            yatt = sb("yatt", [128, 1024], BF16, pc)
            dma('sp', pm[:], pm_d, [], ['pm'], 'd_const')

            fq16 = [fin16, sb("fq16b", [128, 256], BF16, pc)]

            def finish_q(bi, L, par, s0):
                for i in range(2):
                    tr(ptr[:, i * 128:i * 128 + L], fq16[par][0:L, i * 128:(i + 1) * 128], identb[0:L, 0:L],
                       [('fq16', par), 'identb'], ['ptr'], last=(i == 1))
                v3 = ptr[:, 0:256].rearrange("p (j t) -> p j t", j=2)
                q0 = (bi - 8) * 128
                act(Q_FM[:, s0:s0 + 2, q0:q0 + L], v3[:, :, 0:L], AF.Copy, ['ptr'], ['Q_FM'])

            prevs = []
            for u in range(4):
                wb = w_get('q%d' % u)
                m_, jj0 = u // 2, 2 * (u % 2)
                for i0 in range(0, len(own_blocks), 2):
                    pair = own_blocks[i0:i0 + 2]
                    cur = []
                    for j_, bi in enumerate(pair):
                        pb, L = proj_tm(wb, 256, bi)
                        cur.append((pb, L, bi, j_))
                    for p_ in prevs:
                        finish_q(*p_)
                    run_rr([nr_q(pb, L, bi, qnw, 'qnw', fq16[par][0:L, :].rearrange("p (h d) -> p h d", h=4), ('fq16', par))
                            for (pb, L, bi, par) in cur])
                    prevs = [(bi, L, par, m_ * 4 + jj0) for (pb, L, bi, par) in cur]
            for p_ in prevs:
                finish_q(*p_)
            for gg in range(4):
                wb = w_get('g%d' % gg)
                for bi in own_blocks:
                    pb, L = proj_tm(wb, 256, bi)
                    act(sg[0:L, bi - 8, gg * 256:(gg + 1) * 256], pmm[pb][0:L, 0:256], AF.Silu, [('pmm', pb)], ['sg'])

            T.barrier()
            acnt = {'n': 0, 'o': 0}

            def attend(L, qcols, sgb, keyblocks, mask_of, bias_of, ytok):
                for g in range(4):
                    hh, m = g % 2, g // 2
                    ob = 0
                    nkb = len(keyblocks)
                    sis = []

                    def emit_S(ki):
                        kb, Lk = keyblocks[ki]
                        si = acnt['n'] % 6
                        acnt['n'] += 1
                        sis.append(si)
                        pst_, pkey = pss3[si]
                        mm(pst_[0:Lk, 0:4 * L].rearrange("p (j q) -> p j q", j=4),
                           K_FM[hh * 64:(hh + 1) * 64, m, kb * 128:kb * 128 + Lk],
                           Q_FM[hh * 64:(hh + 1) * 64, m * 4:(m + 1) * 4, qcols:qcols + L],
                           True, True, ['K_FM', 'Q_FM'], [pkey], last=True)
                        pv = pt[si][0:Lk, 0:4 * L]
                        act(pv, pst_[0:Lk, 0:4 * L], AF.Exp, [pkey, 'flags', 'zcol'], [('pt', si)],
                            scale=0.125, bias=bias_of(kb, Lk))
                        mk = mask_of(kb, Lk)
                        vtt(pp[si][0:Lk, 0:4 * L].rearrange("p (j q) -> p j q", j=4),
                            pv.rearrange("p (j q) -> p j q", j=4), mk.unsqueeze(1).broadcast_to([Lk, 4, L]),
                            ALU.mult, [('pt', si), 'pm', 'smk'], [('pp', si)])

                    def emit_PV(ki):
                        kb, Lk = keyblocks[ki]
                        si = sis[ki]
                        for j in range(4):
                            mm(pso[ob][0:L, j, :], pp[si][0:Lk, j * L:(j + 1) * L], V_TM[0:Lk, kb, g, :],
                               ki == 0 and j == 0, ki == nkb - 1, [('pp', si), ('V_TM', kb)], [('pso', ob)], last=(j == 3))

                    for k0 in range(min(5, nkb)):
                        emit_S(k0)
                    for ki in range(nkb):
                        if ki + 5 < nkb:
                            emit_S(ki + 5)
                        emit_PV(ki)
                    T.op('dve', lambda: nc.vector.reciprocal(out=rden[0:L, :], in_=pso[ob][0:L, :, 64]),
                         [('pso', ob)], ['rden'])
                    for j in range(4):
                        hq = 4 * g + j
                        vstt(yatt[0:L, hq * 64:(hq + 1) * 64], pso[ob][0:L, j, 0:64], rden[0:L, j:j + 1],
                             sg[0:L, sgb, hq * 64:(hq + 1) * 64], ALU.mult, ALU.mult,
                             [('pso', ob), 'rden', 'sg'], ['yatt'])
                for mc in range(8):
                    tr(ptr[:, mc * 128:mc * 128 + L], yatt[0:L, mc * 128:(mc + 1) * 128], identb[0:L, 0:L],
                       ['yatt', 'identb'], ['ptr'], last=(mc == 7))
                v3 = ptr[:, 0:1024].rearrange("p (c t) -> p c t", c=8)
                vcopy(y_mix[:, 8:16, ytok:ytok + L], v3[:, :, 0:L], ['ptr'], ['y_mix_att'])

            for qb in range(8):
                lb = 8 + qb
                kbs = [(kb, 128) for kb in range(max(0, lb - 16), lb + 1)]
                attend(128, qb * 128, qb, kbs,
                       lambda kb, Lk, lb=lb: pm[0:Lk, lb - kb, :],
                       lambda kb, Lk: (flags[0:Lk, 1:2] if kb < 8 else zcol[0:Lk, :]),
                       qb * 128)
            for kb in range(16):
                dma('pool', V_TM[:, kb, :, 0:64],
                    cache_v[128 * kb:128 * (kb + 1), :].rearrange("p (g d) -> p g d", g=4),
                    [], [('V_TM', kb)], 'd_cache')
            tokv = ('d_cache', T.sems['d_cache'][1])
            for kb in range(16):
                T.last_w[('V_TM', kb)] = tokv
            stage = [(pt[i], ('pt', i)) for i in range(4)] + [(pp[i], ('pp', i)) for i in range(4)]
            for i, (buf_, key_) in enumerate(stage):
                dma('pool', buf_[:, :].rearrange("p (b c) -> p b c", b=2),
                    cache_k[256 * i:256 * (i + 1), :].rearrange("(b p) c -> p b c", p=128),
                    [], [key_], 'd_ck%d' % i)
            for kb in range(16):
                buf_, key_ = stage[kb // 2]
                o_ = (kb % 2) * 256
                for m in range(2):
                    tr(ptr[:, m * 128:(m + 1) * 128], buf_[:, o_ + m * 128:o_ + (m + 1) * 128], identb[:],
                       [key_, 'identb'], ['ptr'], last=(m == 1))
                v3 = ptr[:, 0:256].rearrange("p (m t) -> p m t", m=2)
                if kb % 2 == 0:
                    vcopy(K_FM[:, :, kb * 128:(kb + 1) * 128], v3, ['ptr'], ['K_FM'])
                else:
                    act(K_FM[:, :, kb * 128:(kb + 1) * 128], v3, AF.Copy, ['ptr'], ['K_FM'])
            kbs = [(kb, 128) for kb in range(16)] + [(16, 4)]
            attend(4, 1024, 8, kbs, lambda kb, Lk: smk[0:Lk, kb, :], lambda kb, Lk: zcol[0:Lk, :], 1024)
        T.barrier()

        s4 = es.enter_context(ExitStack())
        sz = sb("sz", [128, 9, 1024], BF16, s4)

        def z_projection():
            for gz in range(4):
                wb = w_get('z%d' % gz)
                for bi in own_blocks:
                    pb, L = proj_tm(wb, 256, bi)
                    act(sz[0:L, bi - 8, gz * 256:(gz + 1) * 256], pmm[pb][0:L, 0:256], AF.Silu, [('pmm', pb)], ['sz'])
                    yield

        with ExitStack() as pd:
            dskip = sb("dskip", [128, 1024], F32, pd); ssdnw = sb("ssdnw", [128, 1024], F32, pd)
            dma('sp', dskip[:], dskip_d, [], ['dskip'], 'd_const')
            dma('sp', ssdnw[:], ssdnw_d, [], ['ssdnw'], 'd_const')
            stl = sb("stl", [128, 128], F32, pd)
            stl4 = sb("stl4", [128, 4, 128], F32, pd)

            def load_state_tiles(g_):
                for i4 in range(4):
                    dma('sp', stl4[:, i4, :], st_ssm[(g_ * 4 + i4) * 128:(g_ * 4 + i4 + 1) * 128, :], [],
                        [('stl4', i4)], 'd_st4_%d' % i4)

            load_state_tiles(0)
            GB = []
            for g_ in range(2):
                t_ = {}
                for nm, shp, dt_ in [('hT', [128, 512], F32), ('hTb', [128, 512], BF16), ('x_tm', [128, 512], BF16),
                                     ('b_tm', [128, 128], BF16), ('xw', [128, 512], BF16), ('xdt', [128, 512], BF16),
                                     ('da', [128, 8], F32), ('cum', [128, 8], F32), ('dif', [128, 8], F32),
                                     ('ee', [128, 8], F32), ('dend', [128, 8], F32), ('cd', [128, 8], F32),
                                     ('wgt', [128, 8], F32), ('cbm', [128, 128], F32), ('Rall', [128, 1024], F32),
                                     ('scall', [128, 1024], BF16), ('t1', [128, 512], F32),
                                     ('yn', [128, 512], BF16), ('ssD', [128, 1], F32), ('rstdD', [128, 1], F32),
                                     ('rsa', [128, 1], F32), ('rsb', [128, 1], F32)]:
                    t_[nm] = sb("%s_g%d" % (nm, g_), shp, dt_, pd)
                GB.append(t_)
                vmemset(t_['ssD'][:, :], 0.0, [('ssD', g_)])
            PB = [
                {'small': (psmall, 'psmall'), 'big': (psob[0], ('pso', 0)), 'pY': (pss[0], ('pss', 0))},
                {'small': (psmall[:, 256:512], 'psmall'), 'big': (psob[1], ('pso', 1)), 'pY': (pss[1], ('pss', 1))},
            ]

            def ssd_chunk(L, bi, g, need_y, yblk, ytok):
                t_ = GB[g]
                K = lambda nm: (nm, g)
                small, SM = PB[g]['small']
                big, BG = PB[g]['big']
                pY_, PY = PB[g]['pY']
                psm_ = small[:, 0:16]
                pcb_ = small[:, 128:256]
                hT_ = t_['hT']
                x_tm, b_tm, xw, xdt = t_['x_tm'], t_['b_tm'], t_['xw'], t_['xdt']
                da, cum, dif, ee, dend, cd, wgt = t_['da'], t_['cum'], t_['dif'], t_['ee'], t_['dend'], t_['cd'], t_['wgt']
                cbm, Rall, scall, t1, yn = t_['cbm'], t_['Rall'], t_['scall'], t_['t1'], t_['yn']
                t2 = Rall[:, 0:512]
                for i in range(4):
                    tr(ptr[0:L, i * 128:(i + 1) * 128], xbc_ap(4 * g + i, bi, L), identb[:],
                       ['xbc', 'identb'], ['ptr'], last=False)
                tr(ptr[0:L, 512:640], xbc_ap(8 + g, bi, L), identb[:], ['xbc', 'identb'], ['ptr'], last=True)
                act(x_tm[0:L, :], ptr[0:L, 0:512], AF.Copy, ['ptr'], [K('x_tm')])
                act(b_tm[0:L, :], ptr[0:L, 512:640], AF.Copy, ['ptr'], [K('b_tm')])
                dtg = dt_tm[0:L, bi, 8 * g:8 * g + 8]
                vtt(da[0:L, :], dtg, abc[0:L, 8 * g:8 * g + 8], ALU.mult, ['dt_tm', 'abc'], [K('da')])
                yield
                mm(psm_[0:L, 0:8], tri[0:L, 0:L], da[0:L, :], True, True, ['tri', K('da')], [SM], last=False)
                mm(psm_[:, 8:16], ones[0:L, :], da[0:L, :], True, True, ['ones', K('da')], [SM], last=True)
                yield
                vcopy(cum[0:L, :], psm_[0:L, 0:8], [SM], [K('cum')])
                vtt(dif[0:L, :], psm_[0:L, 8:16], cum[0:L, :], ALU.subtract, [SM, K('cum')], [K('dif')])
                yield
                act(dend[0:L, :], dif[0:L, :], AF.Exp, [K('dif')], [K('dend')])
                act(cd[:, :], psm_[:, 8:16], AF.Exp, [SM], [K('cd')])
                yield
                vtt(wgt[0:L, :], dend[0:L, :], dtg, ALU.mult, [K('dend'), 'dt_tm'], [K('wgt')])
                vtt(xw[0:L, :].rearrange("p (h d) -> p h d", h=8), x_tm[0:L, :].rearrange("p (h d) -> p h d", h=8),
                    wgt[0:L, :].unsqueeze(2).broadcast_to([L, 8, 64]), ALU.mult, [K('x_tm'), K('wgt')], [K('xw')])
                yield
                if need_y:
                    act(ee[0:L, :], cum[0:L, :], AF.Exp, [K('cum')], [K('ee')])
                    mm(pcb_[0:L, 0:L], xbc_ap(8 + g, bi, L), xbc_ap(10 + g, bi, L), True, True,
                       ['xbc'], [SM], last=True)
                    act(t_['hTb'][:, :], hT_[:, :], AF.Copy, [K('hT')], [K('hTb')])
                    yield
                    vtt(cbm[0:L, 0:L], pcb_[0:L, 0:L], tri[0:L, 0:L], ALU.mult, [SM, 'tri'], [K('cbm')])
                    mm(big[0:L, :], xbc_ap(10 + g, bi, L), t_['hTb'][:, :], True, True,
                       ['xbc', K('hTb')], [BG], last=True)
                    R3 = Rall[0:L, 0:8 * L].rearrange("p (h l) -> p h l", h=8)
                    S3 = scall[0:L, 0:8 * L].rearrange("p (h l) -> p h l", h=8)
                    vtt(R3, tri[0:L, 0:L].unsqueeze(1).broadcast_to([L, 8, L]),
                        da[0:L, :].unsqueeze(2).broadcast_to([L, 8, L]), ALU.mult, ['tri', K('da')], [K('Rall')])
                    yield
                    vtt(t1[0:L, :].rearrange("p (h d) -> p h d", h=8), big[0:L, :].rearrange("p (h d) -> p h d", h=8),
                        ee[0:L, :].unsqueeze(2).broadcast_to([L, 8, 64]), ALU.mult, [BG, K('ee')], [K('t1')])
                    vtt(xdt[0:L, :].rearrange("p (h d) -> p h d", h=8), x_tm[0:L, :].rearrange("p (h d) -> p h d", h=8),
                        dtg.unsqueeze(2).broadcast_to([L, 8, 64]), ALU.mult, [K('x_tm'), 'dt_tm'], [K('xdt')], e='pool')
                    yield
                    for q2 in range(2):
                        mm(big[0:L, 0:4 * L].rearrange("p (h l) -> p h l", h=4), gmat[0:L, 0:L],
                           R3[:, 4 * q2:4 * q2 + 4, :], True, True, ['gmat', K('Rall')], [BG], last=True)
                        yield
                        act(S3[:, 4 * q2:4 * q2 + 4, :], big[0:L, 0:4 * L].rearrange("p (h l) -> p h l", h=4),
                            AF.Exp, [BG], [('scall', g, q2)])
                        yield
                        vtt(S3[:, 4 * q2:4 * q2 + 4, :], S3[:, 4 * q2:4 * q2 + 4, :],
                            cbm[0:L, 0:L].unsqueeze(1).broadcast_to([L, 4, L]), ALU.mult,
                            [('scall', g, q2), K('cbm')], [('scall', g, q2)])
                        yield
                    for h in range(8):
                        mm(pY_[0:L, h * 64:(h + 1) * 64], S3[:, h, :], xdt[0:L, h * 64:(h + 1) * 64], True, True,
                           [('scall', g, h // 4), K('xdt')], [PY], last=(h == 7))
                    yield
                    vtt(t2[0:L, :], x_tm[0:L, :], dskip[0:L, g * 512:(g + 1) * 512], ALU.mult, [K('x_tm'), 'dskip'], [K('Rall')],
                        e='pool')
                    vtt(t1[0:L, :], t1[0:L, :], t2[0:L, :], ALU.add, [K('t1'), K('Rall')], [K('t1')])
                    yield
                    vtt(t1[0:L, :], t1[0:L, :], pY_[0:L, :], ALU.add, [K('t1'), PY], [K('t1')])
                    vtt(t1[0:L, :], t1[0:L, :], sz[0:L, yblk, g * 512:(g + 1) * 512], ALU.mult, [K('t1'), 'sz'], [K('t1')])
                    yield
                    act(yn[0:L, :], t1[0:L, :], AF.Square, [K('t1'), K('ssD')], [K('yn'), K('ssD')],
                        accum_out=t_['ssD'][0:L, :])
                    yield
                    vts(t_['rsa'][0:L, :], t_['ssD'][0:L, :], 1.0 / 512, EPS, ALU.mult, ALU.add, [K('ssD')], [K('rsa')])
                    vmemset(t_['ssD'][:, :], 0.0, [K('ssD')])
                    yield
                    act(t_['rsb'][0:L, :], t_['rsa'][0:L, :], AF.Ln, [K('rsa')], [K('rsb')])
                    act(t_['rstdD'][0:L, :], t_['rsb'][0:L, :], AF.Exp, [K('rsb')], [K('rstdD')], scale=-0.5)
                    yield
                    vstt(yn[0:L, :], t1[0:L, :], t_['rstdD'][0:L, 0:1], ssdnw[0:L, g * 512:(g + 1) * 512], ALU.mult, ALU.mult,
                         [K('t1'), K('rstdD'), 'ssdnw'], [K('yn')])
                    yield
                    for i in range(4):
                        tr(ptr[:, i * 128:i * 128 + L], yn[0:L, i * 128:(i + 1) * 128], identb[0:L, 0:L],
                           [K('yn'), 'identb'], ['ptr'], last=(i == 3))
                    v3 = ptr[:, 0:512].rearrange("p (c t) -> p c t", c=4)
                    act(y_mix[:, 4 * g:4 * g + 4, ytok:ytok + L], v3[:, :, 0:L], AF.Copy, ['ptr'], [('y_mix_ssd', g)])
                mm(big[:, :], b_tm[0:L, :], xw[0:L, :], True, True, [K('b_tm'), K('xw')], [BG], last=True)
                yield
                vtt(hT_[:, :].rearrange("p (h d) -> p h d", h=8), hT_[:, :].rearrange("p (h d) -> p h d", h=8),
                    cd[:, :].unsqueeze(2).broadcast_to([128, 8, 64]), ALU.mult, [K('hT'), K('cd')], [K('hT')])
                vtt(hT_[:, :], hT_[:, :], big[:, :], ALU.add, [K('hT'), BG], [K('hT')])
                yield

            pcbx = pmm[0][:, 0:128]
            PCBX = ('pmm', 0)

            def state_out(dst, g):
                Rg = GB[g]['Rall']
                bufs = [(stl[:, :], ('stl',), 'do_stl')] + \
                       [(Rg[:, k_ * 128:(k_ + 1) * 128], ('stlR', g, k_), 'do_stlR%d_%d' % (g, k_)) for k_ in range(3)]
                T.wait_free('dve', [('Rall', g)])
                for i in range(4):
                    buf_, key_, sem_ = bufs[i]
                    T.op('pe', lambda: nc.tensor.transpose(out=pcbx[:, :], in_=GB[g]['hT'][:, i * 128:(i + 1) * 128],
                                                           identity=identf[:]),
                         [('hT', g), 'identf'], [PCBX], inc=True)
                    vcopy(buf_, pcbx[:, :], [PCBX], [key_])
                    r0 = (g * 4 + i) * 128
                    dma('sp', dst[r0:r0 + 128, :], buf_, [key_], ['ssm_out'], sem_)
                    yield
                T.wait_free('dve', [b_[1] for b_ in bufs[1:]])

            st_loaded = {0: True, 1: False}

            def state_in(g):
                while not st_loaded[g]:
                    yield
                for i in range(4):
                    T.op('pe', lambda: nc.tensor.transpose(out=pcbx[:, :], in_=stl4[:, i, :], identity=identf[:]),
                         [('stl4', i), 'identf'], [PCBX], inc=True)
                    vcopy(GB[g]['hT'][:, i * 128:(i + 1) * 128], pcbx[:, :], [PCBX], [('hT', g)])
                    yield
                if g == 0:
                    load_state_tiles(1)
                    st_loaded[1] = True

            zgen = z_projection()
            zstate = {'done': False}

            def z_step():
                if not zstate['done']:
                    try:
                        next(zgen)
                    except StopIteration:
                        zstate['done'] = True

            def group_prog(g):
                vmemset(GB[g]['hT'][:, :], 0.0, [('hT', g)])
                yield
                for c in range(16):
                    if c == 8:
                        while not zstate['done']:
                            z_step()
                    yield from ssd_chunk(128, c, g, c >= 8, c - 8, (c - 8) * 128)
                    if c == 7:
                        vts(GB[g]['hT'][:, :], GB[g]['hT'][:, :], flags[:, 0:1], None, ALU.mult, None,
                            [('hT', g), 'flags'], [('hT', g)])
                        yield
                yield from state_out(ssm_o, g)
                yield from state_in(g)
                yield from ssd_chunk(4, 16, g, True, 8, 1024)
                yield from state_out(ssms_o, g)

            progs = [group_prog(0), group_prog(1)]
            alive = [True, True]
            while any(alive):
                for gi in range(2):
                    if alive[gi]:
                        try:
                            next(progs[gi])
                        except StopIteration:
                            alive[gi] = False
                z_step()
        T.barrier()
        s4.close()
        s2.close()
        s1.close()

        with ExitStack() as pe_:
            wo = [sb("wo0", [128, 16, 512], BF16, pe_), sb("wo1", [128, 16, 512], BF16, pe_)]
            xr = [sb("xr%d" % i, [128, 512], F32, pe_) for i in range(4)]
            yo = [sb("yo0", [128, 512], F32, pe_), sb("yo1", [128, 512], F32, pe_)]

            def load_wo(i):
                b = i % 2
                v = w_out[:, 512 * (i + 1):512 * (i + 2)].rearrange("(kc p) n -> p kc n", p=128)
                for qd in range(4):
                    dma('pool', wo[b][:, qd * 4:(qd + 1) * 4, :], v[:, qd * 4:(qd + 1) * 4, :],
                        [], [('wo', b, qd)], 'd_wo%d_%d' % (b, qd))

            load_wo(0)
            load_wo(1)
            cnt = 0
            YM = ['y_mix_att', ('y_mix_ssd', 0), ('y_mix_ssd', 1)]
            groups = [('s', 'oa', 0, 256), ('s', 'ob', 256, 256), ('b', 0, 512, 512), ('b', 1, 1024, 512), ('b', 2, 1536, 512)]
            for kind, gi, c0, ncol in groups:
                if kind == 's':
                    wb = w_get(gi)
                for bi in own_blocks:
                    L = 128 if bi < 16 else 4
                    tk = (bi - 8) * 128
                    pb = cnt % 2
                    cnt += 1
                    src = xo[tk:tk + 128, c0:c0 + ncol] if bi < 16 else xsm[:, c0:c0 + ncol]
                    xb_ = (cnt - 1) % 4
                    dma('act', xr[xb_][0:L, 0:ncol], src, [], [('xr', xb_)], 'd_xr%d' % xb_)
                    for mc in range(16):
                        if kind == 's':
                            mm(pmm[pb][0:L, 0:ncol], y_mix[:, mc, tk:tk + L], wst[wb][:, mc, 0:ncol], mc == 0, mc == 15,
                               YM + wkeys(wb, mc), [('pmm', pb)], last=(mc == 15))
                        else:
                            mm(pmm[pb][0:L, :], y_mix[:, mc, tk:tk + L], wo[gi % 2][:, mc, :], mc == 0, mc == 15,
                               YM + [('wo', gi % 2, mc // 4)], [('pmm', pb)], last=(mc == 15))
                    vtt(yo[pb][0:L, 0:ncol], pmm[pb][0:L, 0:ncol], xr[xb_][0:L, 0:ncol], ALU.add,
                        [('pmm', pb), ('xr', xb_)], [('yo', pb)])
                    dst = y_o[tk:tk + 128, c0:c0 + ncol] if bi < 16 else y_s[:, c0:c0 + ncol]
                    dma('sp', dst, yo[pb][0:L, 0:ncol], [('yo', pb)], ['y_out'], 'd_yo%d' % pb)
                if kind == 'b' and gi + 2 < 3:
                    load_wo(gi + 2)
        T.barrier()
        T.final_wait('sp')
        print("kernel instructions (incl waits):", T.nins)
    return nc


_NC_CACHE = {}


def _mult(d):
    d = np.asarray(d)
    m = ((d >= 0) & (d <= 128)).astype(np.float32)
    m += ((d >= 0) & (d <= 512) & (d % 4 == 0))
    m += ((d >= 0) & (d <= 2048) & (d % 16 == 0))
    return m


def kernel(x_prompt, x_sample, cache_k, cache_v, state_conv, state_ssm, norm_w, w_in, conv_w,
           conv_b, dt_bias, a_log, d_skip, ssd_norm_w, q_norm_w, k_norm_w, w_out):
    f32 = np.float32
    A = lambda a: np.ascontiguousarray(np.asarray(a), dtype=f32)
    x_prompt = A(x_prompt); x_sample = A(x_sample); cache_k = A(cache_k); cache_v = A(cache_v)
    state_conv = A(state_conv); state_ssm = A(state_ssm)
    w_in2 = A(w_in)[0]; w_out2 = A(w_out)[0]
    bc = lambda v, n=128: np.ascontiguousarray(np.broadcast_to(A(v).reshape(1, -1), (n, A(v).size)))
    if 'nc' not in _NC_CACHE:
        _NC_CACHE['nc'] = build_nc()
    nc = _NC_CACHE['nc']

    kk = np.arange(128)[:, None, None]; dl = np.arange(17)[None, :, None]; qq = np.arange(128)[None, None, :]
    pm = _mult(128 * dl + qq - kk).astype(ml_dtypes.bfloat16)
    kb = np.arange(17)[None, :, None]; tt = np.arange(4)[None, None, :]
    kidx = 128 * kb + kk
    smk = _mult(2048 + tt - kidx)
    smk = np.where((kidx < 2052), smk, 0.0).astype(ml_dtypes.bfloat16)
    ident = np.eye(128, dtype=f32)
    tri = (np.arange(128)[:, None] <= np.arange(128)[None, :]).astype(f32)
    gmat = (np.arange(128)[:, None] > np.arange(128)[None, :]).astype(f32)
    inv = (500000.0 ** (-np.arange(0, 16, 2, dtype=np.float32) / 16)).astype(f32)
    common = {
        "w_in": w_in2, "w_out": w_out2,
        "normw_bc": bc(norm_w), "qnw_bc": bc(q_norm_w), "knw_bc": bc(k_norm_w),
        "dtb_bc": bc(dt_bias), "alog_bc": bc(a_log),
        "dskip_bc": bc(np.repeat(A(d_skip).reshape(-1), 64)), "ssdnw_bc": bc(ssd_norm_w),
        "convw_fm": np.ascontiguousarray(A(conv_w)[0].reshape(4, 12, 128).transpose(2, 1, 0)),
        "convb_fm": np.ascontiguousarray(A(conv_b)[0].reshape(12, 128).transpose(1, 0)),
        "pm": pm, "smk": smk, "identb": ident.astype(ml_dtypes.bfloat16), "identf": ident,
        "tri": tri, "gmat": gmat,
    }
    in_maps = []
    for c in range(8):
        b, h = c // 2, c % 2
        xo = x_prompt[b, 1024 * h:1024 * (h + 1)]
        xh = x_prompt[b, 0:1024] if h == 1 else np.zeros((1024, D), f32)
        pos = np.empty((17, 128), dtype=np.float64)
        for bi in range(16):
            pos[bi] = 1024 * (h - 1) + bi * 128 + np.arange(128)
        pos[16] = 16384 + (np.arange(128) % 4)
        ang = pos.astype(f32)[:, :, None] * inv[None, None, :]
        fl = np.zeros((128, 4), f32)
        fl[:, 0] = float(h)
        fl[:, 1] = 0.0 if h == 1 else NEG
        m = dict(common)
        m.update({
            "xh": np.ascontiguousarray(xh), "xo": np.ascontiguousarray(xo), "xsm": np.ascontiguousarray(x_sample[c]),
            "cache_k": np.ascontiguousarray(cache_k[0, c].reshape(2048, 256)),
            "cache_v": np.ascontiguousarray(cache_v[0, c].reshape(2048, 256)),
            "sc_fm": np.ascontiguousarray(state_conv[0, c].reshape(3, 12, 128).transpose(2, 1, 0)),
            "st_ssm": np.ascontiguousarray(state_ssm[0, c].reshape(1024, 128)),
            "c_cos": np.ascontiguousarray(np.cos(ang).astype(f32).transpose(1, 0, 2)),
            "c_sin": np.ascontiguousarray(np.sin(ang).astype(f32).transpose(1, 0, 2)),
            "flags": fl,
        })
        in_maps.append(m)
    res = run_bass_kernel_spmd(nc, in_maps, core_ids=list(range(8)))
    R = res.results
    y_prompt = np.zeros((4, 2048, D), f32); k_prompt = np.zeros((1, 4, 2048, 4, 64), f32)
    v_prompt = np.zeros_like(k_prompt)
    conv_prompt = np.zeros((1, 4, 3, 1536), f32); ssm_prompt = np.zeros((1, 4, 16, 64, 128), f32)
    y_sample = np.zeros((8, 4, D), f32); k_sample = np.zeros((1, 8, 2048, 4, 64), f32)
    v_sample = np.zeros_like(k_sample)
    conv_sample = np.zeros((1, 8, 3, 1536), f32); ssm_sample = np.zeros((1, 8, 16, 64, 128), f32)
    for c in range(8):
        b, h = c // 2, c % 2
        r = R[c]
        y_prompt[b, 1024 * h:1024 * (h + 1)] = r["y_o"]
        k_prompt[0, b, 1024 * h:1024 * (h + 1)] = r["k_o"].reshape(1024, 4, 64)
        v_prompt[0, b, 1024 * h:1024 * (h + 1)] = r["v_o"].reshape(1024, 4, 64)
        if h == 1:
            conv_prompt[0, b] = r["conv_o"]
            ssm_prompt[0, b] = r["ssm_o"].reshape(16, 64, 128)
        y_sample[c] = r["y_s"]
        k_sample[0, c] = r["ks_o"].reshape(2048, 4, 64)
        v_sample[0, c] = r["vs_o"].reshape(2048, 4, 64)
        conv_sample[0, c] = r["convs_o"]
        ssm_sample[0, c] = r["ssms_o"].reshape(16, 64, 128)
    return (y_prompt, y_sample, k_prompt, v_prompt, conv_prompt, ssm_prompt,
            k_sample, v_sample, conv_sample, ssm_sample)
```

```python
import numpy as np
import ml_dtypes
from contextlib import ExitStack
import concourse.bass as bass
import concourse.mybir as mybir
from concourse.bass_utils import run_bass_kernel_spmd

F32 = mybir.dt.float32
BF16 = mybir.dt.bfloat16
AF = mybir.ActivationFunctionType
ALU = mybir.AluOpType
AX = mybir.AxisListType

D = 2048
NTOK = 2052
EPS = 1e-6
C_Z, C_XBC, C_DT, C_Q, C_K, C_V, C_G = 0, 1024, 2560, 2576, 3600, 3856, 4112
IN_COLS = 5136
NEG = -30000.0


class Trk:
    def __init__(self, nc, es):
        self.nc = nc
        self.es = es
        self.eng = {'pe': nc.tensor, 'act': nc.scalar, 'dve': nc.vector, 'pool': nc.gpsimd, 'sp': nc.sync}
        self.sems = {}
        self.waited = {e: {} for e in self.eng}
        self.last_w = {}
        self.readers = {}
        self.pending = {e: [] for e in self.eng}
        self.nins = 0
        self.excl = set()

    def sem(self, name):
        if name not in self.sems:
            self.sems[name] = [self.es.enter_context(self.nc.semaphore(name)), 0]
        return self.sems[name]

    def _wait(self, e, tok):
        if tok is None:
            return
        name, val = tok
        if e == 'pe' and name == 'S_pe':
            return
        if self.waited[e].get(name, 0) >= val:
            return
        self.eng[e].wait_ge(self.sems[name][0], val)
        self.waited[e][name] = val
        self.nins += 1

    def _chk_pending(self, e, region):
        for e2, lst in self.pending.items():
            if e2 == e:
                continue
            for (_, r) in lst:
                if r == region:
                    raise RuntimeError(f"region {region} has pending un-signalled op on {e2}")

    def op(self, e, fn, reads=(), writes=(), inc=True, semname=None, incval=1):
        need = {}

        def want(tok, skip_own=False):
            if tok is None:
                return
            n, v = tok
            if e == 'pe' and n == 'S_pe':
                return
            if skip_own and n == 'S_' + e:
                return
            if need.get(n, 0) < v:
                need[n] = v

        xr = [r for r in reads if r in self.excl]
        for r in xr:
            self._chk_pending(e, r)
            want(self.last_w.get(r), True)
            for t in self.readers.get(r, []):
                want(t, True)
        reads = [r for r in reads if r not in self.excl]
        for r in reads:
            self._chk_pending(e, r)
            want(self.last_w.get(r))
        for w in writes:
            self._chk_pending(e, w)
            want(self.last_w.get(w))
            for t in self.readers.get(w, []):
                want(t)
        for n, v in need.items():
            self._wait(e, (n, v))
        writes = list(writes) + xr
        ins = fn()
        self.nins += 1
        if semname is None:
            for r in reads:
                self.pending[e].append(('r', r))
            for w in writes:
                self.pending[e].append(('w', w))
            if inc:
                s = self.sem('S_' + e)
                s[1] += 1
                ins.then_inc(s[0], 1)
                tok = ('S_' + e, s[1])
                for kind, r in self.pending[e]:
                    if kind == 'r':
                        self.readers.setdefault(r, []).append(tok)
                    else:
                        self.last_w[r] = tok
                        self.readers[r] = []
                self.pending[e] = []
        else:
            s = self.sem(semname)
            s[1] += incval
            ins.then_inc(s[0], incval)
            tok = (semname, s[1])
            for r in reads:
                self.readers.setdefault(r, []).append(tok)
            for w in writes:
                self.last_w[w] = tok
                self.readers[w] = []
        return ins

    def wait_free(self, e, keys):
        for k in keys:
            self._wait(e, self.last_w.get(k))
            for t in self.readers.get(k, []):
                self._wait(e, t)

    def barrier(self):
        for e, lst in self.pending.items():
            assert not lst, (e, lst)
        for name, (h, cnt) in list(self.sems.items()):
            if cnt == 0:
                continue
            for e in self.eng:
                self._wait(e, (name, cnt))

    def final_wait(self, e='sp'):
        for name, (h, cnt) in list(self.sems.items()):
            if cnt:
                self._wait(e, (name, cnt))


def build_nc():
    nc = bass.Bass("TRN2", target_bir_lowering=False)
    dram = {}

    def din(name, shape, dt=F32):
        dram[name] = nc.dram_tensor(name, list(shape), dt, kind="ExternalInput").ap()
        return dram[name]

    def dout(name, shape, dt=F32):
        dram[name] = nc.dram_tensor(name, list(shape), dt, kind="ExternalOutput").ap()
        return dram[name]

    xh = din("xh", [1024, D]); xo = din("xo", [1024, D]); xsm = din("xsm", [4, D])
    w_in = din("w_in", [D, IN_COLS]); w_out = din("w_out", [D, D])
    cache_k = din("cache_k", [2048, 256]); cache_v = din("cache_v", [2048, 256])
    sc_fm_d = din("sc_fm", [128, 12, 3]); st_ssm = din("st_ssm", [1024, 128])
    normw_d = din("normw_bc", [128, D]); qnw_d = din("qnw_bc", [128, 64]); knw_d = din("knw_bc", [128, 64])
    cos_d = din("c_cos", [128, 17, 8]); sin_d = din("c_sin", [128, 17, 8])
    dtb_d = din("dtb_bc", [128, 16]); alog_d = din("alog_bc", [128, 16])
    dskip_d = din("dskip_bc", [128, 1024]); ssdnw_d = din("ssdnw_bc", [128, 1024])
    convw_d = din("convw_fm", [128, 12, 4]); convb_d = din("convb_fm", [128, 12])
    pm_d = din("pm", [128, 17, 128], BF16); sm_d = din("smk", [128, 17, 4], BF16)
    flags_d = din("flags", [128, 4])
    identb_d = din("identb", [128, 128], BF16); identf_d = din("identf", [128, 128])
    tri_d = din("tri", [128, 128]); g_d = din("gmat", [128, 128])

    y_o = dout("y_o", [1024, D]); y_s = dout("y_s", [4, D])
    k_o = dout("k_o", [1024, 256]); v_o = dout("v_o", [1024, 256])
    conv_o = dout("conv_o", [3, 1536]); ssm_o = dout("ssm_o", [1024, 128])
    ks_o = dout("ks_o", [2048, 256]); vs_o = dout("vs_o", [2048, 256])
    convs_o = dout("convs_o", [3, 1536]); ssms_o = dout("ssms_o", [1024, 128])

    with ExitStack() as es:
        T = Trk(nc, es)

        def sb(name, shape, dt=F32, stack=es):
            return stack.enter_context(nc.sbuf_tensor("s_" + name, list(shape), dt))

        def ps(name, shape, dt=F32, stack=es):
            return stack.enter_context(nc.psum_tensor("p_" + name, list(shape), dt))

        def dma(q, out, in_, reads, writes, sem, slow=False):
            eng = T.eng[q]
            if slow:
                f = lambda: eng.dma_start(out=out, in_=in_, allow_slow_non_contiguous=True)
            else:
                f = lambda: eng.dma_start(out=out, in_=in_)
            return T.op(q, f, reads, writes, semname=sem, incval=16)

        def mm(out, lhsT, rhs, start, stop, reads, writes, last):
            return T.op('pe', lambda: nc.tensor.matmul(out, lhsT=lhsT, rhs=rhs, start=start, stop=stop),
                        reads, writes, inc=last)

        def tr(out, in_, ident, reads, writes, last):
            return T.op('pe', lambda: nc.tensor.transpose(out=out, in_=in_, identity=ident),
                        reads, writes, inc=last)

        def act(out, in_, func, reads, writes, **kw):
            return T.op('act', lambda: nc.scalar.activation(out=out, in_=in_, func=func, **kw), reads, writes)

        def vtt(out, in0, in1, op, reads, writes, e='dve'):
            return T.op(e, lambda: T.eng[e].tensor_tensor(out=out, in0=in0, in1=in1, op=op), reads, writes)

        def vts(out, in0, s1, s2, op0, op1, reads, writes, e='dve'):
            if op1 is None:
                return T.op(e, lambda: T.eng[e].tensor_scalar(out=out, in0=in0, scalar1=s1, scalar2=None, op0=op0),
                            reads, writes)
            return T.op(e, lambda: T.eng[e].tensor_scalar(out=out, in0=in0, scalar1=s1, scalar2=s2, op0=op0, op1=op1),
                        reads, writes)

        def vstt(out, in0, scalar, in1, op0, op1, reads, writes, e='dve'):
            return T.op(e, lambda: T.eng[e].scalar_tensor_tensor(out=out, in0=in0, scalar=scalar, in1=in1,
                                                                  op0=op0, op1=op1), reads, writes)

        def vcopy(out, in_, reads, writes, e='dve'):
            return T.op(e, lambda: T.eng[e].tensor_copy(out=out, in_=in_), reads, writes)

        def vmemset(ap, val, writes, e='dve'):
            return T.op(e, lambda: T.eng[e].memset(ap, val), (), writes)

        def vreduce(out, in_, reads, writes):
            return T.op('dve', lambda: nc.vector.tensor_reduce(out=out, in_=in_, axis=AX.X, op=ALU.add), reads, writes)

        def rsqrt_chain(dst, ss, n, L, scale, reads_key, tmpa, tmpb):
            vts(tmpa[0:L, 0:n], ss, scale, EPS, ALU.mult, ALU.add, [reads_key], ['rs_a'])
            act(tmpb[0:L, 0:n], tmpa[0:L, 0:n], AF.Ln, ['rs_a'], ['rs_b'])
            act(dst, tmpb[0:L, 0:n], AF.Exp, ['rs_b'], [reads_key + '_r'], scale=-0.5)

        identb = sb("identb", [128, 128], BF16); identf = sb("identf", [128, 128])
        tri = sb("tri", [128, 128]); gmat = sb("gmat", [128, 128]); ones = sb("ones", [128, 128])
        qnw = sb("qnw", [128, 64]); knw = sb("knw", [128, 64])
        ccos = sb("ccos", [128, 17, 8]); csin = sb("csin", [128, 17, 8])
        dtb = sb("dtb", [128, 16]); abc = sb("abc", [128, 16])
        convw = sb("convw", [128, 12, 4]); convb = sb("convb", [128, 12])
        smk = sb("smk", [128, 17, 4], BF16)
        flags = sb("flags", [128, 4]); zcol = sb("zcol", [128, 1])
        scfm = sb("scfm", [128, 12, 3])
        rs_a = sb("rs_a", [128, 16]); rs_b = sb("rs_b", [128, 16])
        y_mix = sb("y_mix", [128, 16, 1028], BF16)
        wraw = [sb("wraw0", [128, 2048], F32), sb("wraw1", [128, 2048], F32)]
        wst = [w_[:].bitcast(BF16).rearrange("p (k n) -> p k n", k=16) for w_ in wraw]
        dt_tm = sb("dt_tm", [128, 17, 16])

        consts = [(identb, identb_d, 'identb'), (identf, identf_d, 'identf'), (tri, tri_d, 'tri'),
                  (gmat, g_d, 'gmat'), (qnw, qnw_d, 'qnw'), (knw, knw_d, 'knw'), (ccos, cos_d, 'rope'),
                  (csin, sin_d, 'rope2'), (dtb, dtb_d, 'dtb'), (abc, alog_d, 'abc'),
                  (convw, convw_d, 'convw'), (convb, convb_d, 'convb'),
                  (smk, sm_d, 'smk'), (flags, flags_d, 'flags'), (scfm, sc_fm_d, 'scfm')]
        for t_, d_, key in consts:
            dma('sp', t_[:], d_, [], [key], 'd_const')
        vmemset(ones[:], 1.0, ['ones'])
        vmemset(zcol[:], 0.0, ['zcol'])
        act(abc[:], abc[:], AF.Exp, ['abc'], ['abc'])
        vts(abc[:], abc[:], -1.0, None, ALU.mult, None, ['abc'], ['abc'])

        wstate = {'n': 0}

        wmode = {}

        def load_w(src, c0, ncols, blocks=None):
            b = wstate['n'] % 2
            wstate['n'] += 1
            allk = [('wst', b, i) for i in range(4)]
            if blocks is None:
                wmode[b] = 'kc'
                T.wait_free('pool', allk)
                v = src[:, c0:c0 + ncols].rearrange("(kc p) n -> p kc n", p=128)
                for qd in range(4):
                    dma('pool', wst[b][:, qd * 4:(qd + 1) * 4, 0:ncols], v[:, qd * 4:(qd + 1) * 4, :],
                        [], [('wst', b, qd)], 'd_w%d_%d' % (b, qd))
            else:
                wmode[b] = 'col'
                T.wait_free('pool', allk)
                for i, cb in enumerate(blocks):
                    v = src[:, cb:cb + 64].rearrange("(kc p) n -> p kc n", p=128)
                    dma('pool', wst[b][:, :, i * 64:(i + 1) * 64], v, [], [('wst', b, i)], 'd_w%d_%d' % (b, i))
            return b

        def wkeys(b, kc):
            if wmode[b] == 'kc':
                return [('wst', b, kc // 4)]
            return [('wst', b, i) for i in range(4)]

        sched = []
        sched.append(('k', w_in, C_K, 256, None))
        sched.append(('v', w_in, C_V, 256, None))
        sched.append(('dt', w_in, C_DT, 16, None))
        for i in range(6):
            sched.append(('x%d' % i, w_in, C_XBC + 256 * i, 256, None))
        for u in range(4):
            m_, jj0 = u // 2, 2 * (u % 2)
            hs = [8 * m_ + jj0, 8 * m_ + 4 + jj0, 8 * m_ + jj0 + 1, 8 * m_ + 4 + jj0 + 1]
            sched.append(('q%d' % u, w_in, None, 256, [C_Q + 64 * h_ for h_ in hs]))
        for i in range(4):
            sched.append(('g%d' % i, w_in, C_G + 256 * i, 256, None))
        for i in range(4):
            sched.append(('z%d' % i, w_in, C_Z + 256 * i, 256, None))
        sched.append(('oa', w_out, 0, 256, None))
        sched.append(('ob', w_out, 256, 256, None))
        wq = {'i': 0, 'buf': {}}

        def w_issue():
            i = wq['i']
            if i < len(sched):
                name, src, c0, ncols, blocks = sched[i]
                wq['buf'][name] = load_w(src, c0, ncols, blocks)
                wq['i'] += 1

        def w_get(name):
            while name not in wq['buf']:
                w_issue()
            w_issue()
            return wq['buf'][name]

        psA = ExitStack()
        ptp = [ps("ptp0", [128, 2048], BF16, psA), ps("ptp1", [128, 2048], BF16, psA)]
        T.excl.update([('ptp', 0, 0), ('ptp', 0, 1), ('ptp', 1, 0), ('ptp', 1, 1)])

        s1 = es.enter_context(ExitStack())
        xbc_h = sb("xbc_h", [128, 10, 1024], BF16, s1)
        xbc_o = sb("xbc_o", [128, 12, 1028], BF16, s1)
        K_FM = sb("K_FM", [128, 2, NTOK], BF16, s1)
        V_TM = sb("V_TM", [128, 17, 4, 65], BF16, s1)
        s2 = es.enter_context(ExitStack())
        hnT_o = sb("hnT_o", [128, 16, 1028], BF16, s2)
        s3 = ExitStack()
        hnT_h = sb("hnT_h", [128, 16, 1024], BF16, s3)

        def hn(kc, bi, L):
            if bi < 8:
                return hnT_h[:, kc, bi * 128:bi * 128 + L]
            return hnT_o[:, kc, (bi - 8) * 128:(bi - 8) * 128 + L]

        def xbc_ap(j, bi, L):
            if bi < 8:
                jj = j if j < 8 else j
                return xbc_h[:, jj, bi * 128:bi * 128 + L]
            return xbc_o[:, j, (bi - 8) * 128:(bi - 8) * 128 + L]

        with ExitStack() as pa:
            normw = sb("normw", [128, D], F32, pa)
            xt = [sb("xt0", [128, D], F32, pa), wraw[0], wraw[1]]
            xn = [sb("xn0", [128, D], BF16, pa), sb("xn1", [128, D], BF16, pa)]
            ssA = [sb("ssA%d" % i, [128, 1], F32, pa) for i in range(2)]
            rstdA = [sb("rstdA%d" % i, [128, 1], F32, pa) for i in range(2)]
            rsA = [sb("rsA%d" % i, [128, 1], F32, pa) for i in range(2)]
            rsB = [sb("rsB%d" % i, [128, 1], F32, pa) for i in range(2)]
            dma('sp', normw[:], normw_d, [], ['normw'], 'd_const')

            def a_stage1(bi):
                L = 128 if bi < 16 else 4
                if bi < 8:
                    src = xh[bi * 128:(bi + 1) * 128, :]
                elif bi < 16:
                    src = xo[(bi - 8) * 128:(bi - 7) * 128, :]
                else:
                    src = xsm
                b = bi % 2
                xb = bi % 3
                dma('sp' if bi % 2 == 0 else 'pool', xt[xb][0:L, :], src, [], [('xt', xb)], 'd_x%d' % xb)
                act(xn[b][0:L, :], xt[xb][0:L, :], AF.Square, [('xt', xb), ('ssA', b)], [('xn', b), ('ssA', b)],
                    accum_out=ssA[b][0:L, :])
                vts(rsA[b][0:L, :], ssA[b][0:L, :], 1.0 / D, EPS, ALU.mult, ALU.add, [('ssA', b)], [('rsA', b)])
                vmemset(ssA[b][:, :], 0.0, [('ssA', b)])
                act(rsB[b][0:L, :], rsA[b][0:L, :], AF.Ln, [('rsA', b)], [('rsB', b)])
                act(rstdA[b][0:L, :], rsB[b][0:L, :], AF.Exp, [('rsB', b)], [('rstdA', b)], scale=-0.5)
                vstt(xn[b][0:L, :], xt[xb][0:L, :], rstdA[b][0:L, 0:1], normw[0:L, :], ALU.mult, ALU.mult,
                     [('xt', xb), ('rstdA', b), 'normw'], [('xn', b)])

            def a_stage2(bi):
                L = 128 if bi < 16 else 4
                b = bi % 2
                for kc in range(16):
                    tr(ptp[b][:, kc * 128:kc * 128 + L], xn[b][0:L, kc * 128:(kc + 1) * 128], identb[0:L, 0:L],
                       [('xn', b), 'identb'], [('ptp', b, kc // 8)], last=(kc % 8 == 7))
                v3 = ptp[b][:].rearrange("p (k t) -> p k t", k=16)
                if bi < 8:
                    d0 = hnT_h[:, 0:8, bi * 128:bi * 128 + L]; d1 = hnT_h[:, 8:16, bi * 128:bi * 128 + L]
                else:
                    o_ = (bi - 8) * 128
                    d0 = hnT_o[:, 0:8, o_:o_ + L]; d1 = hnT_o[:, 8:16, o_:o_ + L]
                act(d0, v3[:, 0:8, 0:L], AF.Copy, [('ptp', b, 0)], [('hnT', bi)])
                vcopy(d1, v3[:, 8:16, 0:L], [('ptp', b, 1)], [('hnT', bi)])

            for b_ in range(2):
                vmemset(ssA[b_][:, :], 0.0, [('ssA', b_)])
            a_stage1(0)
            for bi in range(17):
                if bi + 1 < 17:
                    a_stage1(bi + 1)
                a_stage2(bi)
        T.barrier()
        psA.close()

        pmm = [ps("pmm0", [128, 512]), ps("pmm1", [128, 512])]
        ptr = ps("ptr", [128, 1024], BF16)
        pss = [ps("pss0", [128, 512]), ps("pss1", [128, 512])]
        psob = [ps("pso0", [128, 512]), ps("pso1", [128, 512])]
        pso = [p_[:, 0:260].rearrange("p (j d) -> p j d", j=4) for p_ in psob]
        pY = pss[0]
        pst = pss[1]
        psmall = ps("psmall", [128, 512])
        psm = psmall[:, 0:16]
        pcb = pmm[0][:, 0:128]
        pseg = [psob[0][:, 0:128], psob[1][:, 0:128]]
        PSM, PCB, PSEG = 'psmall', ('pmm', 0), [('pso', 0), ('pso', 1)]
        T.excl.update([('pmm', 0), ('pmm', 1), 'ptr', ('pss', 0), ('pss', 1), ('pso', 0), ('pso', 1), 'psmall'])

        mmcnt = {'n': 0}

        def proj_tm(wb, ncols, bi):
            L = 128 if bi < 16 else 4
            pb = mmcnt['n'] % 2
            mmcnt['n'] += 1
            for kc in range(16):
                mm(pmm[pb][0:L, 0:ncols], hn(kc, bi, L), wst[wb][:, kc, 0:ncols],
                   kc == 0, kc == 15, [('hnT', bi)] + wkeys(wb, kc), [('pmm', pb)], last=(kc == 15))
            return pb, L

        def run_rr(gens):
            gens = list(gens)
            while gens:
                for g_ in list(gens):
                    try:
                        next(g_)
                    except StopIteration:
                        gens.remove(g_)

        def make_norm_rope(stack, tag):
            tm = []
            for par in range(2):
                t_ = {k_: sb("%s_%s%d" % (tag, k_, par), shp, dt_, stack) for k_, shp, dt_ in [
                    ('xw', [128, 256], F32), ('ra', [128, 4, 8], F32), ('rb', [128, 4, 8], F32),
                    ('rc', [128, 4, 8], F32), ('rd', [128, 4, 8], F32), ('ssq', [128, 4], F32),
                    ('rq', [128, 4], F32), ('junk', [128, 64], BF16), ('rsa', [128, 4], F32), ('rsb', [128, 4], F32)]}
                tm.append(t_)
                vmemset(t_['ssq'][:, :], 0.0, [(tag, 'ssq', par)])
            cntr = {'n': 0}

            def norm_rope(pb, L, bi, wbc, wkey, out3, out_key):
                par = cntr['n'] % 2
                cntr['n'] += 1
                yield
                t_ = tm[par]
                K = lambda k_: (tag, k_, par)
                nh = 4
                for h_ in range(nh):
                    act(t_['junk'][0:L, :], pmm[pb][0:L, h_ * 64:(h_ + 1) * 64], AF.Square, [('pmm', pb), K('ssq')],
                        [K('junk'), K('ssq')], accum_out=t_['ssq'][0:L, h_:h_ + 1])
                yield
                x3 = t_['xw'][0:L, :].rearrange("p (h d) -> p h d", h=nh)
                vtt(x3, pmm[pb][0:L, 0:256].rearrange("p (h d) -> p h d", h=nh),
                    wbc[0:L, :].unsqueeze(1).broadcast_to([L, nh, 64]), ALU.mult, [('pmm', pb), wkey], [K('xw')])
                yield
                cosb = ccos[0:L, bi, :].unsqueeze(1).broadcast_to([L, nh, 8])
                sinb = csin[0:L, bi, :].unsqueeze(1).broadcast_to([L, nh, 8])
                vtt(t_['ra'][0:L], x3[:, :, 0:8], cosb, ALU.mult, [K('xw'), 'rope'], [K('ra')])
                vtt(t_['rb'][0:L], x3[:, :, 8:16], sinb, ALU.mult, [K('xw'), 'rope2'], [K('rb')])
                yield
                vtt(t_['rc'][0:L], x3[:, :, 8:16], cosb, ALU.mult, [K('xw'), 'rope'], [K('rc')])
                vtt(t_['rd'][0:L], x3[:, :, 0:8], sinb, ALU.mult, [K('xw'), 'rope2'], [K('rd')])
                yield
                vts(t_['rsa'][0:L, :], t_['ssq'][0:L, :], 1.0 / 64, EPS, ALU.mult, ALU.add, [K('ssq')], [K('rsa')])
                vmemset(t_['ssq'][:, :], 0.0, [K('ssq')])
                yield
                act(t_['rsb'][0:L, :], t_['rsa'][0:L, :], AF.Ln, [K('rsa')], [K('rsb')])
                act(t_['rq'][0:L, :], t_['rsb'][0:L, :], AF.Exp, [K('rsb')], [K('rq')], scale=-0.5)
                yield
                vtt(x3[:, :, 0:8], t_['ra'][0:L], t_['rb'][0:L], ALU.subtract, [K('ra'), K('rb')], [K('xw')])
                yield
                vtt(x3[:, :, 8:16], t_['rc'][0:L], t_['rd'][0:L], ALU.add, [K('rc'), K('rd')], [K('xw')])
                vtt(out3, x3, t_['rq'][0:L, :].unsqueeze(2).broadcast_to([L, nh, 64]), ALU.mult,
                    [K('xw'), K('rq')], [out_key])
                yield
            return norm_rope

        own_blocks = list(range(8, 16)) + [16]

        with ExitStack() as pb1:
            fin = sb("fin", [128, 256], F32, pb1)
            fin16 = sb("fin16", [128, 256], BF16, pb1); vt = sb("vt", [128, 256], F32, pb1)
            rawc = [sb("rawc0", [128, 3 + 2048], F32, pb1), sb("rawc1", [128, 3 + 2048], F32, pb1)]
            raws = sb("raws", [128, 8], F32, pb1)
            cva = sb("cva", [128, 1024], F32, pb1)
            cvs = sb("cvs", [128, 4], F32, pb1)
            nr_k = make_norm_rope(pb1, 'nrk')

            vmemset(V_TM[:].rearrange("p a b c -> p (a b c)"), 1.0, [('V_TM', i_) for i_ in range(17)])
            dma('sp', ks_o[0:2044, :], cache_k[4:2048, :], [], ['ks_o'], 'd_out')
            dma('sp', vs_o[0:2044, :], cache_v[4:2048, :], [], ['vs_o'], 'd_out')
            wb = w_get('k')
            fin2 = [fin, sb("finb", [128, 256], F32, pb1)]
            fin162 = [fin16, sb("fin16b", [128, 256], BF16, pb1)]

            def finish_k(bi, L, par):
                for m in range(2):
                    tr(ptr[:, m * 128:m * 128 + L], fin162[par][0:L, m * 128:(m + 1) * 128], identb[0:L, 0:L],
                       [('fin16', par), 'identb'], ['ptr'], last=(m == 1))
                v3 = ptr[:, 0:256].rearrange("p (m t) -> p m t", m=2)
                act(K_FM[:, :, bi * 128:bi * 128 + L], v3[:, :, 0:L], AF.Copy, ['ptr'], ['K_FM'])

            def k_chain(pb, L, bi, par):
                yield from nr_k(pb, L, bi, knw, 'knw', fin2[par][0:L, :].rearrange("p (h d) -> p h d", h=4), ('fin', par))
                if 8 <= bi < 16:
                    dma('sp', k_o[(bi - 8) * 128:(bi - 7) * 128, :], fin2[par][0:L, :], [('fin', par)], ['k_o'], 'do_fin%d' % par)
                elif bi == 16:
                    dma('sp', ks_o[2044:2048, :], fin2[par][0:L, :], [('fin', par)], ['ks_o'], 'do_fin%d' % par)
                vcopy(fin162[par][0:L, :], fin2[par][0:L, :], [('fin', par)], [('fin16', par)])
                yield

            prevs = []
            kblocks = list(range(17))
            for i0 in range(0, 17, 2):
                pair = kblocks[i0:i0 + 2]
                cur = []
                for j_, bi in enumerate(pair):
                    pb, L = proj_tm(wb, 256, bi)
                    cur.append((pb, L, bi, j_))
                for p_ in prevs:
                    finish_k(*p_)
                run_rr([k_chain(pb, L, bi, par) for (pb, L, bi, par) in cur])
                prevs = [(bi, L, par) for (pb, L, bi, par) in cur]
            for p_ in prevs:
                finish_k(*p_)
            wb = w_get('v')
            vt2 = [vt, fin2[1]]
            for bi in range(17):
                pb, L = proj_tm(wb, 256, bi)
                vb = bi % 2
                vkey = 'vt' if vb == 0 else ('fin', 1)
                act(vt2[vb][0:L, :], pmm[pb][0:L, 0:256], AF.Copy, [('pmm', pb)], [vkey])
                if 8 <= bi < 16:
                    dma('sp', v_o[(bi - 8) * 128:(bi - 7) * 128, :], vt2[vb][0:L, :], [vkey], ['v_o'], 'do_vt%d' % vb)
                elif bi == 16:
                    dma('sp', vs_o[2044:2048, :], vt2[vb][0:L, :], [vkey], ['vs_o'], 'do_vt%d' % vb)
                vcopy(V_TM[0:L, bi, :, 0:64], vt2[vb][0:L, :].rearrange("p (g d) -> p g d", g=4), [vkey], [('V_TM', bi)])
            wb = w_get('dt')
            for bi in range(17):
                pb, L = proj_tm(wb, 16, bi)
                vtt(dt_tm[0:L, bi, :], pmm[pb][0:L, 0:16], dtb[0:L, :], ALU.add, [('pmm', pb), 'dtb'], ['dt_tm'])
                act(dt_tm[0:L, bi, :], dt_tm[0:L, bi, :], AF.Exp, ['dt_tm'], ['dt_tm'])
                act(dt_tm[0:L, bi, :], dt_tm[0:L, bi, :], AF.Ln, ['dt_tm'], ['dt_tm'], bias=1.0)
            def xbc_post(j):
                rb = j % 2
                dma('sp', conv_o.rearrange("k (t p) -> p t k", p=128)[:, j, :], rawc[rb][:, 2048:2051],
                    [('rawc', rb)], ['conv_o'], 'do_rawc%d' % rb, slow=True)
                dma('sp', convs_o.rearrange("k (t p) -> p t k", p=128)[:, j, :], raws[rb][:, 4:7],
                    [('raws', rb)], ['convs_o'], 'do_raws%d' % rb, slow=True)
                for hf in range(2):
                    if hf == 0 and j >= 10:
                        continue
                    o_ = hf * 1024
                    vts(cva[:, :], rawc[rb][:, o_ + 3:o_ + 3 + 1024], convw[:, j, 3:4], convb[:, j:j + 1], ALU.mult, ALU.add,
                        [('rawc', rb), 'convw', 'convb'], ['cva'])
                    for k in (2, 1, 0):
                        vstt(cva[:, :], rawc[rb][:, o_ + k:o_ + k + 1024], convw[:, j, k:k + 1], cva[:, :], ALU.mult, ALU.add,
                             [('rawc', rb), 'convw', 'cva'], ['cva'])
                    if hf == 0:
                        act(xbc_h[:, j, :], cva[:, :], AF.Silu, ['cva'], ['xbc'])
                    else:
                        act(xbc_o[:, j, 0:1024], cva[:, :], AF.Silu, ['cva'], ['xbc'])
                vts(cvs[:, :], raws[rb][:, 3:7], convw[:, j, 3:4], convb[:, j:j + 1], ALU.mult, ALU.add,
                    [('raws', rb), 'convw', 'convb'], ['cvs'])
                for k in (2, 1, 0):
                    vstt(cvs[:, :], raws[rb][:, k:k + 4], convw[:, j, k:k + 1], cvs[:, :], ALU.mult, ALU.add,
                         [('raws', rb), 'convw', 'cvs'], ['cvs'])
                act(xbc_o[:, j, 1024:1028], cvs[:, :], AF.Silu, ['cvs'], ['xbc'])

            raws = [raws, sb("raws_b", [128, 8], F32, pb1)]
            pending_j = None
            for gx in range(6):
                wb = w_get('x%d' % gx)
                for jj in range(2):
                    j = gx * 2 + jj
                    rb = j % 2
                    vmemset(rawc[rb][:, 0:3], 0.0, [('rawc', rb)])
                    for tt in range(4):
                        if j >= 10 and tt == 0:
                            continue
                        pb = mmcnt['n'] % 2
                        mmcnt['n'] += 1
                        for kc in range(16):
                            rhs = hnT_h[:, kc, tt * 512:(tt + 1) * 512] if tt < 2 else hnT_o[:, kc, (tt - 2) * 512:(tt - 1) * 512]
                            mm(pmm[pb][:, :], wst[wb][:, kc, jj * 128:(jj + 1) * 128], rhs,
                               kc == 0, kc == 15, [('hnT', tt * 4 + q_) for q_ in range(4)] + wkeys(wb, kc),
                               [('pmm', pb)], last=(kc == 15))
                        act(rawc[rb][:, 3 + tt * 512:3 + (tt + 1) * 512], pmm[pb][:, :], AF.Copy, [('pmm', pb)], [('rawc', rb)])
                    pb = mmcnt['n'] % 2
                    mmcnt['n'] += 1
                    for kc in range(16):
                        mm(pmm[pb][:, 0:4], wst[wb][:, kc, jj * 128:(jj + 1) * 128], hnT_o[:, kc, 1024:1028],
                           kc == 0, kc == 15, [('hnT', 16)] + wkeys(wb, kc), [('pmm', pb)], last=(kc == 15))
                    vcopy(raws[rb][:, 0:3], scfm[:, j, :], ['scfm'], [('raws', rb)])
                    act(raws[rb][:, 3:7], pmm[pb][:, 0:4], AF.Copy, [('pmm', pb)], [('raws', rb)])
                    if pending_j is not None:
                        xbc_post(pending_j)
                    pending_j = j
            xbc_post(pending_j)
        T.barrier()
        s3.close()

        with ExitStack() as pc:
            Q_FM = sb("Q_FM", [128, 8, 1028], BF16, pc)
            sg = sb("sg", [128, 9, 1024], BF16, pc)
            pm = sb("pm", [128, 17, 128], BF16, pc)
            raw = sb("raw2", [128, 256], F32, pc)
            fin16 = sb("fin162", [128, 256], BF16, pc)
            nr_q = make_norm_rope(pc, 'nrq')
            pt = [sb("pt%d" % i, [128, 512], BF16, pc) for i in range(6)]
            pp = [sb("pp%d" % i, [128, 512], BF16, pc) for i in range(6)]
            pss3 = [(pss[0], ('pss', 0)), (pss[1], ('pss', 1)), (pmm[0], ('pmm', 0)), (pmm[1], ('pmm', 1)),
                    (psmall, 'psmall'), (psob[1], ('pso', 1))]
            rden = sb("rden", [128, 4], F32, pc)
            yatt = sb("yatt", [128, 1024], BF16, pc)
            dma('sp', pm[:], pm_d, [], ['pm'], 'd_const')

            fq16 = [fin16, sb("fq16b", [128, 256], BF16, pc)]

            def finish_q(bi, L, par, s0):
                for i in range(2):
                    tr(ptr[:, i * 128:i * 128 + L], fq16[par][0:L, i * 128:(i + 1) * 128], identb[0:L, 0:L],
                       [('fq16', par), 'identb'], ['ptr'], last=(i == 1))
                v3 = ptr[:, 0:256].rearrange("p (j t) -> p j t", j=2)
                q0 = (bi - 8) * 128
                act(Q_FM[:, s0:s0 + 2, q0:q0 + L], v3[:, :, 0:L], AF.Copy, ['ptr'], ['Q_FM'])

            prevs = []
            for u in range(4):
                wb = w_get('q%d' % u)
                m_, jj0 = u // 2, 2 * (u % 2)
                for i0 in range(0, len(own_blocks), 2):
                    pair = own_blocks[i0:i0 + 2]
                    cur = []
                    for j_, bi in enumerate(pair):
                        pb, L = proj_tm(wb, 256, bi)
                        cur.append((pb, L, bi, j_))
                    for p_ in prevs:
                        finish_q(*p_)
                    run_rr([nr_q(pb, L, bi, qnw, 'qnw', fq16[par][0:L, :].rearrange("p (h d) -> p h d", h=4), ('fq16', par))
                            for (pb, L, bi, par) in cur])
                    prevs = [(bi, L, par, m_ * 4 + jj0) for (pb, L, bi, par) in cur]
            for p_ in prevs:
                finish_q(*p_)
            for gg in range(4):
                wb = w_get('g%d' % gg)
                for bi in own_blocks:
                    pb, L = proj_tm(wb, 256, bi)
                    act(sg[0:L, bi - 8, gg * 256:(gg + 1) * 256], pmm[pb][0:L, 0:256], AF.Silu, [('pmm', pb)], ['sg'])

            T.barrier()
            acnt = {'n': 0, 'o': 0}

            def attend(L, qcols, sgb, keyblocks, mask_of, bias_of, ytok):
                for g in range(4):
                    hh, m = g % 2, g // 2
                    ob = 0
                    nkb = len(keyblocks)
                    sis = []

                    def emit_S(ki):
                        kb, Lk = keyblocks[ki]
                        si = acnt['n'] % 6
                        acnt['n'] += 1
                        sis.append(si)
                        pst_, pkey = pss3[si]
                        mm(pst_[0:Lk, 0:4 * L].rearrange("p (j q) -> p j q", j=4),
                           K_FM[hh * 64:(hh + 1) * 64, m, kb * 128:kb * 128 + Lk],
                           Q_FM[hh * 64:(hh + 1) * 64, m * 4:(m + 1) * 4, qcols:qcols + L],
                           True, True, ['K_FM', 'Q_FM'], [pkey], last=True)
                        pv = pt[si][0:Lk, 0:4 * L]
                        act(pv, pst_[0:Lk, 0:4 * L], AF.Exp, [pkey, 'flags', 'zcol'], [('pt', si)],
                            scale=0.125, bias=bias_of(kb, Lk))
                        mk = mask_of(kb, Lk)
                        vtt(pp[si][0:Lk, 0:4 * L].rearrange("p (j q) -> p j q", j=4),
                            pv.rearrange("p (j q) -> p j q", j=4), mk.unsqueeze(1).broadcast_to([Lk, 4, L]),
                            ALU.mult, [('pt', si), 'pm', 'smk'], [('pp', si)])

                    def emit_PV(ki):
                        kb, Lk = keyblocks[ki]
                        si = sis[ki]
                        for j in range(4):
                            mm(pso[ob][0:L, j, :], pp[si][0:Lk, j * L:(j + 1) * L], V_TM[0:Lk, kb, g, :],
                               ki == 0 and j == 0, ki == nkb - 1, [('pp', si), ('V_TM', kb)], [('pso', ob)], last=(j == 3))

                    for k0 in range(min(5, nkb)):
                        emit_S(k0)
                    for ki in range(nkb):
                        if ki + 5 < nkb:
                            emit_S(ki + 5)
                        emit_PV(ki)
                    T.op('dve', lambda: nc.vector.reciprocal(out=rden[0:L, :], in_=pso[ob][0:L, :, 64]),
                         [('pso', ob)], ['rden'])
                    for j in range(4):
                        hq = 4 * g + j
                        vstt(yatt[0:L, hq * 64:(hq + 1) * 64], pso[ob][0:L, j, 0:64], rden[0:L, j:j + 1],
                             sg[0:L, sgb, hq * 64:(hq + 1) * 64], ALU.mult, ALU.mult,
                             [('pso', ob), 'rden', 'sg'], ['yatt'])
                for mc in range(8):
                    tr(ptr[:, mc * 128:mc * 128 + L], yatt[0:L, mc * 128:(mc + 1) * 128], identb[0:L, 0:L],
                       ['yatt', 'identb'], ['ptr'], last=(mc == 7))
                v3 = ptr[:, 0:1024].rearrange("p (c t) -> p c t", c=8)
                act(y_mix[:, 8:16, ytok:ytok + L], v3[:, :, 0:L], AF.Copy, ['ptr'], ['y_mix_att'])

            for qb in range(8):
                lb = 8 + qb
                kbs = [(kb, 128) for kb in range(max(0, lb - 16), lb + 1)]
                attend(128, qb * 128, qb, kbs,
                       lambda kb, Lk, lb=lb: pm[0:Lk, lb - kb, :],
                       lambda kb, Lk: (flags[0:Lk, 1:2] if kb < 8 else zcol[0:Lk, :]),
                       qb * 128)
            for kb in range(16):
                dma('pool', V_TM[:, kb, :, 0:64],
                    cache_v[128 * kb:128 * (kb + 1), :].rearrange("p (g d) -> p g d", g=4),
                    [], [('V_TM', kb)], 'd_cache')
            tokv = ('d_cache', T.sems['d_cache'][1])
            for kb in range(16):
                T.last_w[('V_TM', kb)] = tokv
            stage = [(pt[i], ('pt', i)) for i in range(4)] + [(pp[i], ('pp', i)) for i in range(4)]
            for i, (buf_, key_) in enumerate(stage):
                dma('pool', buf_[:, :].rearrange("p (b c) -> p b c", b=2),
                    cache_k[256 * i:256 * (i + 1), :].rearrange("(b p) c -> p b c", p=128),
                    [], [key_], 'd_ck%d' % i)
            for kb in range(16):
                buf_, key_ = stage[kb // 2]
                o_ = (kb % 2) * 256
                for m in range(2):
                    tr(ptr[:, m * 128:(m + 1) * 128], buf_[:, o_ + m * 128:o_ + (m + 1) * 128], identb[:],
                       [key_, 'identb'], ['ptr'], last=(m == 1))
                v3 = ptr[:, 0:256].rearrange("p (m t) -> p m t", m=2)
                if kb % 2 == 0:
                    vcopy(K_FM[:, :, kb * 128:(kb + 1) * 128], v3, ['ptr'], ['K_FM'])
                else:
                    act(K_FM[:, :, kb * 128:(kb + 1) * 128], v3, AF.Copy, ['ptr'], ['K_FM'])
            kbs = [(kb, 128) for kb in range(16)] + [(16, 4)]
            attend(4, 1024, 8, kbs, lambda kb, Lk: smk[0:Lk, kb, :], lambda kb, Lk: zcol[0:Lk, :], 1024)
        T.barrier()

        s4 = es.enter_context(ExitStack())
        sz = sb("sz", [128, 9, 1024], BF16, s4)

        def z_projection():
            for gz in range(4):
                wb = w_get('z%d' % gz)
                for bi in own_blocks:
                    pb, L = proj_tm(wb, 256, bi)
                    act(sz[0:L, bi - 8, gz * 256:(gz + 1) * 256], pmm[pb][0:L, 0:256], AF.Silu, [('pmm', pb)], ['sz'])
                    yield

        with ExitStack() as pd:
            dskip = sb("dskip", [128, 1024], F32, pd); ssdnw = sb("ssdnw", [128, 1024], F32, pd)
            dma('sp', dskip[:], dskip_d, [], ['dskip'], 'd_const')
            dma('sp', ssdnw[:], ssdnw_d, [], ['ssdnw'], 'd_const')
            stl = sb("stl", [128, 128], F32, pd)
            stl4 = sb("stl4", [128, 4, 128], F32, pd)

            def load_state_tiles(g_):
                for i4 in range(4):
                    dma('sp', stl4[:, i4, :], st_ssm[(g_ * 4 + i4) * 128:(g_ * 4 + i4 + 1) * 128, :], [],
                        [('stl4', i4)], 'd_st4_%d' % i4)

            load_state_tiles(0)
            GB = []
            for g_ in range(2):
                t_ = {}
                for nm, shp, dt_ in [('hT', [128, 512], F32), ('hTb', [128, 512], BF16), ('x_tm', [128, 512], BF16),
                                     ('b_tm', [128, 128], BF16), ('xw', [128, 512], BF16), ('xdt', [128, 512], BF16),
                                     ('da', [128, 8], F32), ('cum', [128, 8], F32), ('dif', [128, 8], F32),
                                     ('ee', [128, 8], F32), ('dend', [128, 8], F32), ('cd', [128, 8], F32),
                                     ('wgt', [128, 8], F32), ('cbm', [128, 128], F32), ('Rall', [128, 1024], F32),
                                     ('scall', [128, 1024], BF16), ('t1', [128, 512], F32),
                                     ('yn', [128, 512], BF16), ('ssD', [128, 1], F32), ('rstdD', [128, 1], F32),
                                     ('rsa', [128, 1], F32), ('rsb', [128, 1], F32)]:
                    t_[nm] = sb("%s_g%d" % (nm, g_), shp, dt_, pd)
                GB.append(t_)
                vmemset(t_['ssD'][:, :], 0.0, [('ssD', g_)])
            PB = [
                {'small': (psmall, 'psmall'), 'big': (psob[0], ('pso', 0)), 'pY': (pss[0], ('pss', 0))},
                {'small': (psmall[:, 256:512], 'psmall'), 'big': (psob[1], ('pso', 1)), 'pY': (pss[1], ('pss', 1))},
            ]

            def ssd_chunk(L, bi, g, need_y, yblk, ytok):
                t_ = GB[g]
                K = lambda nm: (nm, g)
                small, SM = PB[g]['small']
                big, BG = PB[g]['big']
                pY_, PY = PB[g]['pY']
                psm_ = small[:, 0:16]
                pcb_ = small[:, 128:256]
                hT_ = t_['hT']
                x_tm, b_tm, xw, xdt = t_['x_tm'], t_['b_tm'], t_['xw'], t_['xdt']
                da, cum, dif, ee, dend, cd, wgt = t_['da'], t_['cum'], t_['dif'], t_['ee'], t_['dend'], t_['cd'], t_['wgt']
                cbm, Rall, scall, t1, yn = t_['cbm'], t_['Rall'], t_['scall'], t_['t1'], t_['yn']
                t2 = Rall[:, 0:512]
                for i in range(4):
                    tr(ptr[0:L, i * 128:(i + 1) * 128], xbc_ap(4 * g + i, bi, L), identb[:],
                       ['xbc', 'identb'], ['ptr'], last=False)
                tr(ptr[0:L, 512:640], xbc_ap(8 + g, bi, L), identb[:], ['xbc', 'identb'], ['ptr'], last=True)
                act(x_tm[0:L, :], ptr[0:L, 0:512], AF.Copy, ['ptr'], [K('x_tm')])
                act(b_tm[0:L, :], ptr[0:L, 512:640], AF.Copy, ['ptr'], [K('b_tm')])
                dtg = dt_tm[0:L, bi, 8 * g:8 * g + 8]
                vtt(da[0:L, :], dtg, abc[0:L, 8 * g:8 * g + 8], ALU.mult, ['dt_tm', 'abc'], [K('da')])
                yield
                mm(psm_[0:L, 0:8], tri[0:L, 0:L], da[0:L, :], True, True, ['tri', K('da')], [SM], last=False)
                mm(psm_[:, 8:16], ones[0:L, :], da[0:L, :], True, True, ['ones', K('da')], [SM], last=True)
                yield
                vcopy(cum[0:L, :], psm_[0:L, 0:8], [SM], [K('cum')])
                vtt(dif[0:L, :], psm_[0:L, 8:16], cum[0:L, :], ALU.subtract, [SM, K('cum')], [K('dif')])
                yield
                act(dend[0:L, :], dif[0:L, :], AF.Exp, [K('dif')], [K('dend')])
                act(cd[:, :], psm_[:, 8:16], AF.Exp, [SM], [K('cd')])
                yield
                vtt(wgt[0:L, :], dend[0:L, :], dtg, ALU.mult, [K('dend'), 'dt_tm'], [K('wgt')])
                vtt(xw[0:L, :].rearrange("p (h d) -> p h d", h=8), x_tm[0:L, :].rearrange("p (h d) -> p h d", h=8),
                    wgt[0:L, :].unsqueeze(2).broadcast_to([L, 8, 64]), ALU.mult, [K('x_tm'), K('wgt')], [K('xw')])
                yield
                if need_y:
                    act(ee[0:L, :], cum[0:L, :], AF.Exp, [K('cum')], [K('ee')])
                    mm(pcb_[0:L, 0:L], xbc_ap(8 + g, bi, L), xbc_ap(10 + g, bi, L), True, True,
                       ['xbc'], [SM], last=True)
                    act(t_['hTb'][:, :], hT_[:, :], AF.Copy, [K('hT')], [K('hTb')])
                    yield
                    vtt(cbm[0:L, 0:L], pcb_[0:L, 0:L], tri[0:L, 0:L], ALU.mult, [SM, 'tri'], [K('cbm')])
                    mm(big[0:L, :], xbc_ap(10 + g, bi, L), t_['hTb'][:, :], True, True,
                       ['xbc', K('hTb')], [BG], last=True)
                    R3 = Rall[0:L, 0:8 * L].rearrange("p (h l) -> p h l", h=8)
                    S3 = scall[0:L, 0:8 * L].rearrange("p (h l) -> p h l", h=8)
                    vtt(R3, tri[0:L, 0:L].unsqueeze(1).broadcast_to([L, 8, L]),
                        da[0:L, :].unsqueeze(2).broadcast_to([L, 8, L]), ALU.mult, ['tri', K('da')], [K('Rall')])
                    yield
                    vtt(t1[0:L, :].rearrange("p (h d) -> p h d", h=8), big[0:L, :].rearrange("p (h d) -> p h d", h=8),
                        ee[0:L, :].unsqueeze(2).broadcast_to([L, 8, 64]), ALU.mult, [BG, K('ee')], [K('t1')])
                    vtt(xdt[0:L, :].rearrange("p (h d) -> p h d", h=8), x_tm[0:L, :].rearrange("p (h d) -> p h d", h=8),
                        dtg.unsqueeze(2).broadcast_to([L, 8, 64]), ALU.mult, [K('x_tm'), 'dt_tm'], [K('xdt')], e='pool')
                    yield
                    for q2 in range(2):
                        mm(big[0:L, 0:4 * L].rearrange("p (h l) -> p h l", h=4), gmat[0:L, 0:L],
                           R3[:, 4 * q2:4 * q2 + 4, :], True, True, ['gmat', K('Rall')], [BG], last=True)
                        yield
                        act(S3[:, 4 * q2:4 * q2 + 4, :], big[0:L, 0:4 * L].rearrange("p (h l) -> p h l", h=4),
                            AF.Exp, [BG], [('scall', g, q2)])
                        yield
                        vtt(S3[:, 4 * q2:4 * q2 + 4, :], S3[:, 4 * q2:4 * q2 + 4, :],
                            cbm[0:L, 0:L].unsqueeze(1).broadcast_to([L, 4, L]), ALU.mult,
                            [('scall', g, q2), K('cbm')], [('scall', g, q2)])
                        yield
                    for h in range(8):
                        mm(pY_[0:L, h * 64:(h + 1) * 64], S3[:, h, :], xdt[0:L, h * 64:(h + 1) * 64], True, True,
                           [('scall', g, h // 4), K('xdt')], [PY], last=(h == 7))
                    yield
                    vtt(t2[0:L, :], x_tm[0:L, :], dskip[0:L, g * 512:(g + 1) * 512], ALU.mult, [K('x_tm'), 'dskip'], [K('Rall')],
                        e='pool')
                    vtt(t1[0:L, :], t1[0:L, :], t2[0:L, :], ALU.add, [K('t1'), K('Rall')], [K('t1')])
                    yield
                    vtt(t1[0:L, :], t1[0:L, :], pY_[0:L, :], ALU.add, [K('t1'), PY], [K('t1')])
                    vtt(t1[0:L, :], t1[0:L, :], sz[0:L, yblk, g * 512:(g + 1) * 512], ALU.mult, [K('t1'), 'sz'], [K('t1')])
                    yield
                    act(yn[0:L, :], t1[0:L, :], AF.Square, [K('t1'), K('ssD')], [K('yn'), K('ssD')],
                        accum_out=t_['ssD'][0:L, :])
                    yield
                    vts(t_['rsa'][0:L, :], t_['ssD'][0:L, :], 1.0 / 512, EPS, ALU.mult, ALU.add, [K('ssD')], [K('rsa')])
                    vmemset(t_['ssD'][:, :], 0.0, [K('ssD')])
                    yield
                    act(t_['rsb'][0:L, :], t_['rsa'][0:L, :], AF.Ln, [K('rsa')], [K('rsb')])
                    act(t_['rstdD'][0:L, :], t_['rsb'][0:L, :], AF.Exp, [K('rsb')], [K('rstdD')], scale=-0.5)
                    yield
                    vstt(yn[0:L, :], t1[0:L, :], t_['rstdD'][0:L, 0:1], ssdnw[0:L, g * 512:(g + 1) * 512], ALU.mult, ALU.mult,
                         [K('t1'), K('rstdD'), 'ssdnw'], [K('yn')])
                    yield
                    for i in range(4):
                        tr(ptr[:, i * 128:i * 128 + L], yn[0:L, i * 128:(i + 1) * 128], identb[0:L, 0:L],
                           [K('yn'), 'identb'], ['ptr'], last=(i == 3))
                    v3 = ptr[:, 0:512].rearrange("p (c t) -> p c t", c=4)
                    act(y_mix[:, 4 * g:4 * g + 4, ytok:ytok + L], v3[:, :, 0:L], AF.Copy, ['ptr'], [('y_mix_ssd', g)])
                mm(big[:, :], b_tm[0:L, :], xw[0:L, :], True, True, [K('b_tm'), K('xw')], [BG], last=True)
                yield
                vtt(hT_[:, :].rearrange("p (h d) -> p h d", h=8), hT_[:, :].rearrange("p (h d) -> p h d", h=8),
                    cd[:, :].unsqueeze(2).broadcast_to([128, 8, 64]), ALU.mult, [K('hT'), K('cd')], [K('hT')])
                vtt(hT_[:, :], hT_[:, :], big[:, :], ALU.add, [K('hT'), BG], [K('hT')])
                yield

            pcbx = pmm[0][:, 0:128]
            PCBX = ('pmm', 0)

            def state_out(dst, g):
                Rg = GB[g]['Rall']
                bufs = [(stl[:, :], ('stl',), 'do_stl')] + \
                       [(Rg[:, k_ * 128:(k_ + 1) * 128], ('stlR', g, k_), 'do_stlR%d_%d' % (g, k_)) for k_ in range(3)]
                T.wait_free('dve', [('Rall', g)])
                for i in range(4):
                    buf_, key_, sem_ = bufs[i]
                    T.op('pe', lambda: nc.tensor.transpose(out=pcbx[:, :], in_=GB[g]['hT'][:, i * 128:(i + 1) * 128],
                                                           identity=identf[:]),
                         [('hT', g), 'identf'], [PCBX], inc=True)
                    vcopy(buf_, pcbx[:, :], [PCBX], [key_])
                    r0 = (g * 4 + i) * 128
                    dma('sp', dst[r0:r0 + 128, :], buf_, [key_], ['ssm_out'], sem_)
                    yield
                T.wait_free('dve', [b_[1] for b_ in bufs[1:]])

            st_loaded = {0: True, 1: False}

            def state_in(g):
                while not st_loaded[g]:
                    yield
                for i in range(4):
                    T.op('pe', lambda: nc.tensor.transpose(out=pcbx[:, :], in_=stl4[:, i, :], identity=identf[:]),
                         [('stl4', i), 'identf'], [PCBX], inc=True)
                    vcopy(GB[g]['hT'][:, i * 128:(i + 1) * 128], pcbx[:, :], [PCBX], [('hT', g)])
                    yield
                if g == 0:
                    load_state_tiles(1)
                    st_loaded[1] = True

            zgen = z_projection()
            zstate = {'done': False}

            def z_step():
                if not zstate['done']:
                    try:
                        next(zgen)
                    except StopIteration:
                        zstate['done'] = True

            def group_prog(g):
                vmemset(GB[g]['hT'][:, :], 0.0, [('hT', g)])
                yield
                for c in range(16):
                    if c == 8:
                        while not zstate['done']:
                            z_step()
                    yield from ssd_chunk(128, c, g, c >= 8, c - 8, (c - 8) * 128)
                    if c == 7:
                        vts(GB[g]['hT'][:, :], GB[g]['hT'][:, :], flags[:, 0:1], None, ALU.mult, None,
                            [('hT', g), 'flags'], [('hT', g)])
                        yield
                yield from state_out(ssm_o, g)
                yield from state_in(g)
                yield from ssd_chunk(4, 16, g, True, 8, 1024)
                yield from state_out(ssms_o, g)

            progs = [group_prog(0), group_prog(1)]
            alive = [True, True]
            while any(alive):
                for gi in range(2):
                    if alive[gi]:
                        try:
                            next(progs[gi])
                        except StopIteration:
                            alive[gi] = False
                z_step()
        T.barrier()
        s4.close()
        s2.close()
        s1.close()

        with ExitStack() as pe_:
            wo = [sb("wo0", [128, 16, 512], BF16, pe_), sb("wo1", [128, 16, 512], BF16, pe_)]
            xr = [sb("xr%d" % i, [128, 512], F32, pe_) for i in range(4)]
            yo = [sb("yo0", [128, 512], F32, pe_), sb("yo1", [128, 512], F32, pe_)]

            def load_wo(i):
                b = i % 2
                v = w_out[:, 512 * (i + 1):512 * (i + 2)].rearrange("(kc p) n -> p kc n", p=128)
                for qd in range(4):
                    dma('pool', wo[b][:, qd * 4:(qd + 1) * 4, :], v[:, qd * 4:(qd + 1) * 4, :],
                        [], [('wo', b, qd)], 'd_wo%d_%d' % (b, qd))

            load_wo(0)
            load_wo(1)
            cnt = 0
            YM = ['y_mix_att', ('y_mix_ssd', 0), ('y_mix_ssd', 1)]
            groups = [('s', 'oa', 0, 256), ('s', 'ob', 256, 256), ('b', 0, 512, 512), ('b', 1, 1024, 512), ('b', 2, 1536, 512)]
            for kind, gi, c0, ncol in groups:
                if kind == 's':
                    wb = w_get(gi)
                for bi in own_blocks:
                    L = 128 if bi < 16 else 4
                    tk = (bi - 8) * 128
                    pb = cnt % 2
                    cnt += 1
                    src = xo[tk:tk + 128, c0:c0 + ncol] if bi < 16 else xsm[:, c0:c0 + ncol]
                    xb_ = (cnt - 1) % 4
                    dma('act', xr[xb_][0:L, 0:ncol], src, [], [('xr', xb_)], 'd_xr%d' % xb_)
                    for mc in range(16):
                        if kind == 's':
                            mm(pmm[pb][0:L, 0:ncol], y_mix[:, mc, tk:tk + L], wst[wb][:, mc, 0:ncol], mc == 0, mc == 15,
                               YM + wkeys(wb, mc), [('pmm', pb)], last=(mc == 15))
                        else:
                            mm(pmm[pb][0:L, :], y_mix[:, mc, tk:tk + L], wo[gi % 2][:, mc, :], mc == 0, mc == 15,
                               YM + [('wo', gi % 2, mc // 4)], [('pmm', pb)], last=(mc == 15))
                    vtt(yo[pb][0:L, 0:ncol], pmm[pb][0:L, 0:ncol], xr[xb_][0:L, 0:ncol], ALU.add,
                        [('pmm', pb), ('xr', xb_)], [('yo', pb)])
                    dst = y_o[tk:tk + 128, c0:c0 + ncol] if bi < 16 else y_s[:, c0:c0 + ncol]
                    dma('sp', dst, yo[pb][0:L, 0:ncol], [('yo', pb)], ['y_out'], 'd_yo%d' % pb)
                if kind == 'b' and gi + 2 < 3:
                    load_wo(gi + 2)
        T.barrier()
        T.final_wait('sp')
        print("kernel instructions (incl waits):", T.nins)
    return nc


_NC_CACHE = {}


def _mult(d):
    d = np.asarray(d)
    m = ((d >= 0) & (d <= 128)).astype(np.float32)
    m += ((d >= 0) & (d <= 512) & (d % 4 == 0))
    m += ((d >= 0) & (d <= 2048) & (d % 16 == 0))
    return m


def kernel(x_prompt, x_sample, cache_k, cache_v, state_conv, state_ssm, norm_w, w_in, conv_w,
           conv_b, dt_bias, a_log, d_skip, ssd_norm_w, q_norm_w, k_norm_w, w_out):
    f32 = np.float32
    A = lambda a: np.ascontiguousarray(np.asarray(a), dtype=f32)
    x_prompt = A(x_prompt); x_sample = A(x_sample); cache_k = A(cache_k); cache_v = A(cache_v)
    state_conv = A(state_conv); state_ssm = A(state_ssm)
    w_in2 = A(w_in)[0]; w_out2 = A(w_out)[0]
    bc = lambda v, n=128: np.ascontiguousarray(np.broadcast_to(A(v).reshape(1, -1), (n, A(v).size)))
    if 'nc' not in _NC_CACHE:
        _NC_CACHE['nc'] = build_nc()
    nc = _NC_CACHE['nc']

    kk = np.arange(128)[:, None, None]; dl = np.arange(17)[None, :, None]; qq = np.arange(128)[None, None, :]
    pm = _mult(128 * dl + qq - kk).astype(ml_dtypes.bfloat16)
    kb = np.arange(17)[None, :, None]; tt = np.arange(4)[None, None, :]
    kidx = 128 * kb + kk
    smk = _mult(2048 + tt - kidx)
    smk = np.where((kidx < 2052), smk, 0.0).astype(ml_dtypes.bfloat16)
    ident = np.eye(128, dtype=f32)
    tri = (np.arange(128)[:, None] <= np.arange(128)[None, :]).astype(f32)
    gmat = (np.arange(128)[:, None] > np.arange(128)[None, :]).astype(f32)
    inv = (500000.0 ** (-np.arange(0, 16, 2, dtype=np.float32) / 16)).astype(f32)
    common = {
        "w_in": w_in2, "w_out": w_out2,
        "normw_bc": bc(norm_w), "qnw_bc": bc(q_norm_w), "knw_bc": bc(k_norm_w),
        "dtb_bc": bc(dt_bias), "alog_bc": bc(a_log),
        "dskip_bc": bc(np.repeat(A(d_skip).reshape(-1), 64)), "ssdnw_bc": bc(ssd_norm_w),
        "convw_fm": np.ascontiguousarray(A(conv_w)[0].reshape(4, 12, 128).transpose(2, 1, 0)),
        "convb_fm": np.ascontiguousarray(A(conv_b)[0].reshape(12, 128).transpose(1, 0)),
        "pm": pm, "smk": smk, "identb": ident.astype(ml_dtypes.bfloat16), "identf": ident,
        "tri": tri, "gmat": gmat,
    }
    in_maps = []
    for c in range(8):
        b, h = c // 2, c % 2
        xo = x_prompt[b, 1024 * h:1024 * (h + 1)]
        xh = x_prompt[b, 0:1024] if h == 1 else np.zeros((1024, D), f32)
        pos = np.empty((17, 128), dtype=np.float64)
        for bi in range(16):
            pos[bi] = 1024 * (h - 1) + bi * 128 + np.arange(128)
        pos[16] = 16384 + (np.arange(128) % 4)
        ang = pos.astype(f32)[:, :, None] * inv[None, None, :]
        fl = np.zeros((128, 4), f32)
        fl[:, 0] = float(h)
        fl[:, 1] = 0.0 if h == 1 else NEG
        m = dict(common)
        m.update({
            "xh": np.ascontiguousarray(xh), "xo": np.ascontiguousarray(xo), "xsm": np.ascontiguousarray(x_sample[c]),
            "cache_k": np.ascontiguousarray(cache_k[0, c].reshape(2048, 256)),
            "cache_v": np.ascontiguousarray(cache_v[0, c].reshape(2048, 256)),
            "sc_fm": np.ascontiguousarray(state_conv[0, c].reshape(3, 12, 128).transpose(2, 1, 0)),
            "st_ssm": np.ascontiguousarray(state_ssm[0, c].reshape(1024, 128)),
            "c_cos": np.ascontiguousarray(np.cos(ang).astype(f32).transpose(1, 0, 2)),
            "c_sin": np.ascontiguousarray(np.sin(ang).astype(f32).transpose(1, 0, 2)),
            "flags": fl,
        })
        in_maps.append(m)
    res = run_bass_kernel_spmd(nc, in_maps, core_ids=list(range(8)))
    R = res.results
    y_prompt = np.zeros((4, 2048, D), f32); k_prompt = np.zeros((1, 4, 2048, 4, 64), f32)
    v_prompt = np.zeros_like(k_prompt)
    conv_prompt = np.zeros((1, 4, 3, 1536), f32); ssm_prompt = np.zeros((1, 4, 16, 64, 128), f32)
    y_sample = np.zeros((8, 4, D), f32); k_sample = np.zeros((1, 8, 2048, 4, 64), f32)
    v_sample = np.zeros_like(k_sample)
    conv_sample = np.zeros((1, 8, 3, 1536), f32); ssm_sample = np.zeros((1, 8, 16, 64, 128), f32)
    for c in range(8):
        b, h = c // 2, c % 2
        r = R[c]
        y_prompt[b, 1024 * h:1024 * (h + 1)] = r["y_o"]
        k_prompt[0, b, 1024 * h:1024 * (h + 1)] = r["k_o"].reshape(1024, 4, 64)
        v_prompt[0, b, 1024 * h:1024 * (h + 1)] = r["v_o"].reshape(1024, 4, 64)
        if h == 1:
            conv_prompt[0, b] = r["conv_o"]
            ssm_prompt[0, b] = r["ssm_o"].reshape(16, 64, 128)
        y_sample[c] = r["y_s"]
        k_sample[0, c] = r["ks_o"].reshape(2048, 4, 64)
        v_sample[0, c] = r["vs_o"].reshape(2048, 4, 64)
        conv_sample[0, c] = r["convs_o"]
        ssm_sample[0, c] = r["ssms_o"].reshape(16, 64, 128)
    return (y_prompt, y_sample, k_prompt, v_prompt, conv_prompt, ssm_prompt,
            k_sample, v_sample, conv_sample, ssm_sample)
```
